# Optimizing a Trainium2 kernel written in Bass

```python
import math
import jax, jax.numpy as jnp
from jax import lax
import numpy as np

D_MODEL = 2048
BATCH = 16
SEQ = 256
DEPTH = 2
DEC_BATCH = 4
DEC_SEQ = 1024
PAST_LEN = 256

GRID_W = 64
D_A = D_MODEL // 2
D_B = D_MODEL // 2
HGRN_EXPAND = 128
H_A = D_A // HGRN_EXPAND
DK_A = HGRN_EXPAND
DV_A = D_A // H_A
CHUNK = 32
HY_ORDER = 2
HY_EMB = 33
HY_BANDS = (HY_EMB - 1) // 2
HY_HID = 64
HY_SIN_W = 1.0
D_FF = 5632
N_MOD = 6 * D_MODEL
N_PROJ = 5 * D_A + 3 * D_B + 2 * D_MODEL
PROJ_SPLITS = (D_A, 2 * D_A, 3 * D_A, 4 * D_A, 5 * D_A, 5 * D_A + 3 * D_B, 5 * D_A + 3 * D_B + D_MODEL)
EPS = 1e-6

kernel_name = 'hgrn2_hyena_gated_prefix_diffusion_step'


def _rmsnorm(x, g):
    xf = x.astype(jnp.float32)
    y = xf * lax.rsqrt(jnp.mean(xf * xf, axis=-1, keepdims=True) + EPS)
    return y.astype(x.dtype) * g


def _dwconv_seq(x, w, b):
    L = x.shape[1]
    xp = jnp.pad(x, ((0, 0), (1, 1), (0, 0)))
    return xp[:, :L] * w[0] + xp[:, 1:L + 1] * w[1] + xp[:, 2:] * w[2] + b


def _dwconv_grid(x, w, b):
    bsz, L, C = x.shape
    rows = L // GRID_W
    xg = x.reshape(bsz, rows, GRID_W, C)
    y = lax.conv_general_dilated(xg, w[:, :, None, :], (1, 1), 'SAME',
                                 dimension_numbers=('NHWC', 'HWIO', 'NHWC'),
                                 feature_group_count=C)
    return y.reshape(bsz, L, C) + b


def _chunk_scan(q, k, v, logf, s0):
    bsz, L, H, _ = q.shape
    dv = v.shape[-1]
    n = L // CHUNK

    def to_chunks(t):
        return t.reshape(bsz, n, CHUNK, H, t.shape[-1]).transpose(1, 0, 3, 2, 4)

    tri = jnp.tril(jnp.ones((CHUNK, CHUNK), dtype=bool))[:, :, None]

    def step(s, inp):
        qc, kc, vc, lc = inp
        b = jnp.cumsum(lc, axis=2)
        o_inter = jnp.einsum('bhtd,bhde->bhte', qc * jnp.exp(b), s)
        diff = b[:, :, :, None, :] - b[:, :, None, :, :]
        decay = jnp.exp(jnp.where(tri, diff, -jnp.inf))
        att = jnp.einsum('bhtd,bhsd,bhtsd->bhts', qc, kc, decay)
        o = o_inter + jnp.einsum('bhts,bhse->bhte', att, vc)
        b_last = b[:, :, -1:, :]
        s_new = jnp.exp(b_last[:, :, 0, :])[..., None] * s + jnp.einsum(
            'bhsd,bhse->bhde', kc * jnp.exp(b_last - b), vc)
        return s_new, o

    s_fin, o = lax.scan(step, s0.astype(jnp.float32),
                        (to_chunks(q), to_chunks(k), to_chunks(v), to_chunks(logf)))
    o = o.transpose(1, 0, 3, 2, 4).reshape(bsz, L, H, dv)
    return o, s_fin


def _hgrn_mixer(q_raw, fa_raw, fb_raw, i_raw, g_raw, lb, norm_w, s0):
    bsz, L, _ = q_raw.shape
    f32 = jnp.float32
    q = jax.nn.silu(q_raw.astype(f32)).reshape(bsz, L, H_A, DK_A)
    v = i_raw.astype(f32).reshape(bsz, L, H_A, DV_A)
    o_sum = None
    states = []
    for d, (f_raw, rev) in enumerate(((fa_raw, False), (fb_raw, True))):
        z = f_raw.astype(f32)
        lbd = lb[d].astype(f32)
        logf = jnp.logaddexp(jnp.log(lbd), jnp.log1p(-lbd) + jax.nn.log_sigmoid(z))
        k = (1.0 - lbd) * jax.nn.sigmoid(-z)
        logf = logf.reshape(bsz, L, H_A, DK_A)
        k = k.reshape(bsz, L, H_A, DK_A)
        if rev:
            o_d, s_d = _chunk_scan(q[:, ::-1], k[:, ::-1], v[:, ::-1], logf[:, ::-1], s0[:, d])
            o_d = o_d[:, ::-1]
        else:
            o_d, s_d = _chunk_scan(q, k, v, logf, s0[:, d])
        o_sum = o_d if o_sum is None else o_sum + o_d
        states.append(s_d)
    o = _rmsnorm(o_sum, norm_w.astype(f32)).reshape(bsz, L, D_A)
    o = o * jax.nn.silu(g_raw.astype(f32))
    return o.astype(q_raw.dtype), jnp.stack(states, axis=1)


def _hyena_filters(L, w1, b1, w2, b2, w3, decay):
    f32 = jnp.float32
    t = jnp.arange(L, dtype=f32)
    t01 = t / max(L - 1, 1)
    bands = jnp.linspace(1e-4, HY_BANDS - 1, HY_BANDS, dtype=f32)
    ang = (2.0 * math.pi / L) * t[:, None] * bands[None, :]
    feats = jnp.concatenate([t01[:, None], jnp.cos(ang), -jnp.sin(ang)], axis=-1)
    h = jnp.sin(HY_SIN_W * (feats @ w1.astype(f32) + b1.astype(f32)))
    h = jnp.sin(HY_SIN_W * (h @ w2.astype(f32) + b2.astype(f32)))
    h = (h @ w3.astype(f32)).reshape(L, HY_ORDER, 2, D_B)
    window = jnp.exp(-t01[:, None, None] * jnp.abs(decay.astype(f32))[None])
    h = h * window[:, :, None, :]
    hf, hb = h[:, :, 0], h[:, :, 1]
    zero = jnp.zeros_like(hf[:1])
    k2 = jnp.concatenate([hf[:1] + hb[:1], hf[1:], zero, hb[1:][::-1]], axis=0)
    k2 = k2 / (jnp.sum(jnp.abs(k2), axis=0, keepdims=True) + EPS)
    return jnp.fft.rfft(k2, axis=0)


def _fftconv(z, kf, bias):
    L = z.shape[1]
    zf = jnp.fft.rfft(z, n=2 * L, axis=1)
    y = jnp.fft.irfft(zf * kf[None], n=2 * L, axis=1)[:, :L]
    return y + z * bias


def _hyena_mixer(u, conv_w, conv_b, w1, b1, w2, b2, w3, decay, bias):
    L = u.shape[1]
    uc = _dwconv_seq(u, conv_w, conv_b).astype(jnp.float32)
    v, x1, x2 = jnp.split(uc, 3, axis=-1)
    kf = _hyena_filters(L, w1, b1, w2, b2, w3, decay)
    bias = bias.astype(jnp.float32)
    z = x1 * _fftconv(v, kf[:, 0], bias[0])
    y = x2 * _fftconv(z, kf[:, 1], bias[1])
    return y.astype(u.dtype)


def _layer(x, cond, s0, lb, grid, lp):
    mod = jax.nn.silu(cond) @ lp['w_mod'] + lp['b_mod']
    sh1, sc1, gt1, sh2, sc2, gt2 = [m[:, None, :] for m in jnp.split(mod, 6, axis=-1)]
    h = _rmsnorm(x, lp['g_pre_mix']) * (1.0 + sc1) + sh1
    proj = h @ lp['w_in']
    q_raw, fa_raw, fb_raw, i_raw, g_raw, hy_in, gate_a, gate_b = jnp.split(proj, PROJ_SPLITS, axis=-1)
    o_a, s_fin = _hgrn_mixer(q_raw, fa_raw, fb_raw, i_raw, g_raw, lb, lp['hgrn_norm'], s0)
    o_b = _hyena_mixer(hy_in, lp['hy_conv_w'], lp['hy_conv_b'], lp['hy_w1'], lp['hy_b1'],
                       lp['hy_w2'], lp['hy_b2'], lp['hy_w3'], lp['hy_decay'], lp['hy_bias'])
    merged = (jax.nn.sigmoid(gate_a) * (o_a @ lp['w_branch_a'])
              + jax.nn.sigmoid(gate_b) * (o_b @ lp['w_branch_b']))
    x = x + gt1 * _rmsnorm(merged @ lp['w_out'], lp['g_post_mix'])
    h = _rmsnorm(x, lp['g_pre_ffn']) * (1.0 + sc2) + sh2
    u = h @ lp['ffn_w_up']
    if grid:
        u = _dwconv_grid(u, lp['ffn_conv_w'], lp['ffn_conv_b'])
    else:
        u = _dwconv_seq(u, lp['ffn_conv_w'][1], lp['ffn_conv_b'])
    a, vv = jnp.split(u, 2, axis=-1)
    x = x + gt2 * _rmsnorm((jax.nn.silu(a) * vv) @ lp['ffn_w_down'], lp['g_post_ffn'])
    return x, s_fin


def setup_inputs(seed: int = 0) -> dict:
    key = jax.random.key(seed)
    ks = jax.random.split(key, 32)
    f32 = jnp.float32
    D = D_MODEL

    def nrm(k, shape, scale):
        return jax.random.normal(k, shape, f32) * scale

    decay_base = jnp.asarray(np.linspace(math.log(1e2) / 1.5, math.log(1e2) / 0.3, D_B), dtype=f32)
    return {
        'x_prompt': nrm(ks[0], (BATCH, SEQ, D), 1.0),
        'x_sample': nrm(ks[1], (DEC_BATCH, DEC_SEQ, D), 1.0),
        'state_hgrn': nrm(ks[2], (DEC_BATCH, DEPTH, 2, H_A, DK_A, DV_A), 0.5),
        'c': nrm(ks[3], (DEC_BATCH, D), 1.0),
        'c_ctx': nrm(ks[4], (D,), 1.0),
        'w_mod': nrm(ks[5], (DEPTH, D, N_MOD), 0.5 * D ** -0.5),
        'b_mod': nrm(ks[6], (DEPTH, N_MOD), 0.01),
        'g_pre_mix': 1.0 + nrm(ks[7], (DEPTH, D), 0.05),
        'g_post_mix': 1.0 + nrm(ks[8], (DEPTH, D), 0.05),
        'g_pre_ffn': 1.0 + nrm(ks[9], (DEPTH, D), 0.05),
        'g_post_ffn': 1.0 + nrm(ks[10], (DEPTH, D), 0.05),
        'w_in': nrm(ks[11], (DEPTH, D, N_PROJ), D ** -0.5),
        'hgrn_lower_bounds': nrm(ks[12], (DEPTH, 2, D_A), 0.1),
        'hgrn_norm': 1.0 + nrm(ks[13], (DEPTH, DV_A), 0.05),
        'hy_conv_w': nrm(ks[14], (DEPTH, 3, 3 * D_B), 0.5),
        'hy_conv_b': nrm(ks[15], (DEPTH, 3 * D_B), 0.01),
        'hy_w1': nrm(ks[16], (DEPTH, HY_EMB, HY_HID), HY_EMB ** -0.5),
        'hy_b1': nrm(ks[17], (DEPTH, HY_HID), 0.1),
        'hy_w2': nrm(ks[18], (DEPTH, HY_HID, HY_HID), HY_HID ** -0.5),
        'hy_b2': nrm(ks[19], (DEPTH, HY_HID), 0.1),
        'hy_w3': nrm(ks[20], (DEPTH, HY_HID, HY_ORDER * 2 * D_B), HY_HID ** -0.5),
        'hy_decay': decay_base * (1.0 + nrm(ks[21], (DEPTH, HY_ORDER, D_B), 0.1)),
        'hy_bias': nrm(ks[22], (DEPTH, HY_ORDER, D_B), 0.5),
        'w_branch_a': nrm(ks[23], (DEPTH, D_A, D), D_A ** -0.5),
        'w_branch_b': nrm(ks[24], (DEPTH, D_B, D), D_B ** -0.5),
        'w_out': nrm(ks[25], (DEPTH, D, D), D ** -0.5),
        'ffn_w_up': nrm(ks[26], (DEPTH, D, 2 * D_FF), D ** -0.5),
        'ffn_conv_w': nrm(ks[27], (DEPTH, 3, 3, 2 * D_FF), 1.0 / 3.0),
        'ffn_conv_b': nrm(ks[28], (DEPTH, 2 * D_FF), 0.01),
        'ffn_w_down': nrm(ks[29], (DEPTH, D_FF, D), D_FF ** -0.5),
    }


def reference(x_prompt, x_sample, state_hgrn, c, c_ctx, w_mod, b_mod, g_pre_mix, g_post_mix,
              g_pre_ffn, g_post_ffn, w_in, hgrn_lower_bounds, hgrn_norm, hy_conv_w, hy_conv_b,
              hy_w1, hy_b1, hy_w2, hy_b2, hy_w3, hy_decay, hy_bias, w_branch_a, w_branch_b,
              w_out, ffn_w_up, ffn_conv_w, ffn_conv_b, ffn_w_down):
    p_lb = jax.nn.softmax(hgrn_lower_bounds.astype(jnp.float32), axis=0)
    cs = jnp.cumsum(p_lb, axis=0)
    lbs = cs - cs[:1]

    y_prompt = x_prompt
    y_sample = x_sample
    s0_ctx = jnp.zeros((x_prompt.shape[0], 2, H_A, DK_A, DV_A), jnp.float32)
    cond_ctx = c_ctx[None, :]
    new_states = []
    for l in range(DEPTH):
        lp = {
            'w_mod': w_mod[l], 'b_mod': b_mod[l],
            'g_pre_mix': g_pre_mix[l], 'g_post_mix': g_post_mix[l],
            'g_pre_ffn': g_pre_ffn[l], 'g_post_ffn': g_post_ffn[l],
            'w_in': w_in[l], 'hgrn_norm': hgrn_norm[l],
            'hy_conv_w': hy_conv_w[l], 'hy_conv_b': hy_conv_b[l],
            'hy_w1': hy_w1[l], 'hy_b1': hy_b1[l], 'hy_w2': hy_w2[l], 'hy_b2': hy_b2[l],
            'hy_w3': hy_w3[l], 'hy_decay': hy_decay[l], 'hy_bias': hy_bias[l],
            'w_branch_a': w_branch_a[l], 'w_branch_b': w_branch_b[l], 'w_out': w_out[l],
            'ffn_w_up': ffn_w_up[l], 'ffn_conv_w': ffn_conv_w[l], 'ffn_conv_b': ffn_conv_b[l],
            'ffn_w_down': ffn_w_down[l],
        }
        y_prompt, st_ctx = _layer(y_prompt, cond_ctx, s0_ctx, lbs[l], False, lp)
        new_states.append(st_ctx)
        y_sample, _ = _layer(y_sample, c, state_hgrn[:, l], lbs[l], True, lp)
    new_state_hgrn = jnp.stack(new_states, axis=1).astype(x_prompt.dtype)
    return (y_prompt, y_sample, new_state_hgrn)
```

```python
import math
from contextlib import ExitStack
import numpy as np
import concourse.bass as bass
import concourse.mybir as mybir
from concourse.bass_utils import run_bass_kernel_spmd

F32 = mybir.dt.float32
BF16 = mybir.dt.bfloat16
AF = mybir.ActivationFunctionType
ALU = mybir.AluOpType

D = 2048
T = 1024
DEPTH = 2
D_A = 1024
NPROJ = 12288
DFF = 5632
NFC = 44
EPS = 1e-6
PAD = 66


class Buf:
    __slots__ = ("name", "w", "r")

    def __init__(self, name=""):
        self.name = name
        self.w = None
        self.r = {}


class Prog:
    ENG = ("pe", "act", "dve", "pool", "sp")

    def __init__(self, nc):
        self.nc = nc
        self.lists = {k: [] for k in self.ENG}
        self.cnt = {k: 0 for k in self.ENG}
        self.seen = {k: {} for k in self.ENG}
        self.semh = {}
        self.dma_pool = {}
        self.dma_next = {}
        self.dma_val = {}
        self.n_dma_sems = {"sp": 24, "pool": 16, "act": 4}

    def alloc_sems(self, stack):
        for k in self.ENG:
            self.semh[k] = stack.enter_context(self.nc.semaphore("s_" + k))
        for q, n in self.n_dma_sems.items():
            keys = []
            for i in range(n):
                key = "d_%s_%d" % (q, i)
                self.semh[key] = stack.enter_context(self.nc.semaphore(key))
                self.dma_val[key] = 0
                keys.append(key)
            self.dma_pool[q] = keys
            self.dma_next[q] = 0

    def _deps(self, e, reads, writes, acc):
        deps = {}

        def add(ev):
            if ev is None:
                return
            k, v = ev
            if deps.get(k, 0) < v:
                deps[k] = v

        for b in reads:
            add(b.w)
        for b in writes:
            if not (acc and b.w is not None and b.w[0] == e):
                add(b.w)
            for k, v in b.r.items():
                if not (acc and k == e):
                    add((k, v))
        waits = []
        seen = self.seen[e]
        for k, v in deps.items():
            if seen.get(k, 0) < v:
                seen[k] = v
                waits.append((k, v))
        return waits

    def op(self, e, fn, reads=(), writes=(), acc=False):
        waits = self._deps(e, reads, writes, acc)
        self.cnt[e] += 1
        v = self.cnt[e]
        self.lists[e].append((waits, fn, (e, 1)))
        for b in reads:
            if b.r.get(e, 0) < v:
                b.r[e] = v
        for b in writes:
            b.w = (e, v)
            b.r = {}

    def dma(self, q, fn, reads=(), writes=()):
        pool = self.dma_pool[q]
        key = pool[self.dma_next[q] % len(pool)]
        self.dma_next[q] += 1
        waits = self._deps(q, reads, writes, False)
        pv = self.dma_val[key]
        if pv > 0 and self.seen[q].get(key, 0) < pv:
            self.seen[q][key] = pv
            waits.append((key, pv))
        v = pv + 16
        self.dma_val[key] = v
        self.lists[q].append((waits, fn, (key, 16)))
        for b in reads:
            if b.r.get(key, 0) < v:
                b.r[key] = v
        for b in writes:
            b.w = (key, v)
            b.r = {}

    def wait_all(self, e, bufs):
        deps = {}
        for b in bufs:
            if b.w is not None:
                deps[b.w[0]] = max(deps.get(b.w[0], 0), b.w[1])
            for k, v in b.r.items():
                deps[k] = max(deps.get(k, 0), v)
        waits = [(k, v) for k, v in deps.items() if self.seen[e].get(k, 0) < v]
        for k, v in waits:
            self.seen[e][k] = v
        self.lists[e].append((waits, None, None))

    def emit(self):
        nc = self.nc
        with nc.Block() as block:
            def mk(e):
                def body(engine):
                    for waits, fn, inc in self.lists[e]:
                        for k, v in waits:
                            engine.wait_ge(self.semh[k], v)
                        if fn is not None:
                            fn(engine).then_inc(self.semh[inc[0]], inc[1])
                return body
            block.tensor(mk("pe"))
            block.scalar(mk("act"))
            block.vector(mk("dve"))
            block.gpsimd(mk("pool"))
            block.sync(mk("sp"))


def build(n_layers=DEPTH, dbg=None):
    nc = bass.Bass("TRN2", target_bir_lowering=False)
    dbg = dbg or {}

    def din(name, shape):
        return nc.dram_tensor(name, list(shape), F32, kind="ExternalInput").ap()

    def dout(name, shape):
        return nc.dram_tensor(name, list(shape), F32, kind="ExternalOutput").ap()

    xT_d = din("xT", [D, T])
    yT_d = dout("yT", [D, T])
    st_d = dout("st", [DEPTH, 4, 2, 8, 128, 128])
    s0_d = din("s0", [DEPTH, 2, 8, 128, 128])
    cond_d = din("cond", [128, 16])
    cfg_d = din("cfg", [128, 8])
    rowt_d = din("rowt", [128, 8, 4])
    tmask_d = din("tmask", [4, T])
    feats_d = din("featsT", [33, T])
    MC_d = din("MC", [T, T]); MS_d = din("MS", [T, T]); FC_d = din("FC", [T, T]); FS_d = din("FS", [T, T])
    trif_d = din("trif", [128, 128]); trib_d = din("trib", [128, 128])
    w_mod_d = din("w_mod", [DEPTH, D, 6 * D]); b_mod_d = din("b_modT", [DEPTH, 128, 96])
    gvec_d = din("gvec", [DEPTH, 128, 4, 16])
    w_in_d = din("w_in", [DEPTH, D, NPROJ])
    hlb_d = din("hlb", [DEPTH, 128, 16])
    hnorm_d = din("hnorm", [DEPTH, 128, 1])
    hcw_d = din("hcw", [DEPTH, 128, 24, 3]); hcb_d = din("hcb", [DEPTH, 128, 24])
    hw1_d = din("hw1", [DEPTH, 33, 64]); hb1_d = din("hb1", [DEPTH, 64, 1])
    hw2_d = din("hw2", [DEPTH, 64, 64]); hb2_d = din("hb2", [DEPTH, 64, 1])
    hw3_d = din("hw3", [DEPTH, 64, 4096])
    hdec_d = din("hdec", [DEPTH, 2, 1024]); hbias_d = din("hbias", [DEPTH, 128, 2, 8])
    wba_d = din("wba", [DEPTH, 1024, D]); wbb_d = din("wbb", [DEPTH, 1024, D])
    wout_d = din("wout", [DEPTH, D, D])
    wup_d = din("wup", [DEPTH, D, 2 * DFF])
    fcw_d = din("fcw", [DEPTH, 128, 88, 9]); fcb_d = din("fcb", [DEPTH, 128, 88])
    wdn_d = din("wdn", [DEPTH, DFF, D])
    xs_d = nc.dram_tensor("xspill", [128, 16, T], F32).ap()
    dbg_d = {k: dout("dbg_" + k, shp) for k, shp in dbg.items()}

    with ExitStack() as st:
        P = Prog(nc)
        P.alloc_sems(st)

        def sb(name, shape, dt=F32):
            return st.enter_context(nc.sbuf_tensor("sb_" + name, list(shape), dt))

        big = sb("big", [128, 16, T], F32)
        b_big = [Buf("big%d" % c) for c in range(16)]
        hT = sb("hT", [128, 16, T], BF16)
        b_hT = [Buf("hT%d" % c) for c in range(16)]
        NW = 3
        wring = [sb("wr%d" % i, [128, 4096], BF16) for i in range(NW)]
        b_wr = [Buf("wr%d" % i) for i in range(NW)]
        wctr = [0]
        tmpn_t = sb("tmpn", [128, T], F32)
        xt_t = sb("xt", [128, T], F32)
        b_arena = Buf("tmpn"); b_xt = Buf("xt")
        ar2 = sb("ar2", [128, 24576], BF16)
        UW = T + 2 * PAD
        Uvar = ar2[:, 0:3 * UW].rearrange("p (a b) -> p a b", b=UW); b_U = Buf("Uvar")
        o_ = 3 * UW + (-(3 * UW) % 64)
        diag = [ar2[:, o_ + i * 1152:o_ + (i + 1) * 1152].rearrange("p (a b) -> p a b", b=128) for i in range(2)]; b_dg = [Buf("dg0"), Buf("dg1")]
        o_ += 2304
        sa_t = ar2[:, o_:o_ + 2048].bitcast(F32); b_sa = Buf("sa")
        o_ += 2048
        gT = [ar2[:, o_ + i * 2048:o_ + (i + 1) * 2048].rearrange("p (a b) -> p a b", b=T) for i in range(2)]; b_gT = [Buf("gT0"), Buf("gT1")]
        MC = ar2[:, 0:8192].rearrange("p (a b) -> p a b", b=T); MS = ar2[:, 8192:16384].rearrange("p (a b) -> p a b", b=T)
        b_MC = Buf("MC")
        merged = ar2[:, 0:16384].rearrange("p (a b) -> p a b", b=T); b_mg = [Buf("mg%d" % i) for i in range(16)]
        obT = ar2[:, 16384:24576].rearrange("p (a b) -> p a b", b=T); b_ob = [Buf("ob%d" % i) for i in range(8)]
        bigf = big[:].rearrange("p a b -> p (a b)")
        b_R1 = Buf("R1")

        def cf32(off, n):
            return bigf[:, off:off + n]

        def cbf(off, n):
            return bigf[:, off:off + n // 2].bitcast(BF16)

        ident_b = sb("ident_b", [128, 128], BF16); ident_f = sb("ident_f", [128, 128], F32)
        ones_b = sb("ones_b", [128, 128], BF16)
        b_const = Buf("const")
        cfg = sb("cfg", [128, 8]); rowt = sb("rowt", [128, 8, 4])
        tmask = sb("tmask", [128, 4, T], BF16)
        trif = sb("trif", [128, 128]); trib = sb("trib", [128, 128])
        scond = sb("scond", [128, 16], BF16)
        condt = sb("condt", [128, 16])
        modT = sb("modT", [128, 96]); b_mod = Buf("modT")
        bmodT = sb("bmodT", [128, 96])
        gvec = sb("gvec", [128, 4, 16])
        AB = sb("AB", [128, 6, 16]); b_AB = Buf("AB")
        lbt = sb("lbt", [128, 3, 16]); b_lbt = Buf("lbt")
        hlb = sb("hlb", [128, 2, 16])
        hnorm = sb("hnorm", [128, 1])
        hcw = sb("hcw", [128, 24, 3]); hcb = sb("hcb", [128, 24]); hcwc = sb("hcwc", [128, 24, 2])
        hbias = sb("hbias", [128, 2, 8])
        fcw = sb("fcw", [128, 88, 9]); fcb = sb("fcb", [128, 88])
        b_lw = Buf("layerw")
        rstd = sb("rstd", [128, T]); b_rstd = Buf("rstd")
        sq = [sb("sq%d" % i, [128, T], BF16) for i in range(2)]; b_sq = [Buf("sq0"), Buf("sq1")]
        sqc = [0]

        pst = [st.enter_context(nc.psum_tensor("ps%d" % i, [128, 1024], F32)) for i in range(4)]
        b_ps = [Buf("ps%d" % i) for i in range(4)]

        E = lambda name: name

        def V(fn, r=(), w=(), acc=False):
            P.op("dve", fn, r, w, acc)

        def A(fn, r=(), w=(), acc=False):
            P.op("act", fn, r, w, acc)

        def G(fn, r=(), w=(), acc=False):
            P.op("pool", fn, r, w, acc)

        def M(fn, r=(), w=(), acc=True):
            P.op("pe", fn, r, w, acc)

        def ld(dst, src, w, q="sp", r=()):
            P.dma(q, lambda e: e.dma_start(out=dst, in_=src), reads=r, writes=w)

        def wload(src_ap, shape):
            i = wctr[0] % NW
            wctr[0] += 1
            n = 1
            for s in shape[1:]:
                n *= s
            assert n <= 4096
            flat = wring[i][:, 0:n]
            if len(shape) == 3:
                view = flat.rearrange("p (a b) -> p a b", b=shape[2])
            else:
                view = flat
            P.dma("pool", lambda e: e.dma_start(out=view, in_=src_ap), writes=[b_wr[i]])
            return view, b_wr[i]

        def kview(w2d, c0, c1, KC):
            return w2d.rearrange("(c p) n -> p c n", p=128)[:, 0:KC, c0:c1]

        G(lambda e: e.memset(ones_b[:], 1.0), w=[b_const])
        G(lambda e: e.memset(ident_f[:], 1.0), w=[b_const])
        G(lambda e: e.affine_select(out=ident_f[:], in_=ident_f[:], pattern=[[-1, 128]], compare_op=ALU.is_equal,
                                    fill=0.0, base=0, channel_multiplier=1), r=[b_const], w=[b_const])
        G(lambda e: e.tensor_copy(out=ident_b[:], in_=ident_f[:]), r=[b_const], w=[b_const])
        ld(cfg[:], cfg_d, [b_const]); ld(rowt[:], rowt_d, [b_const])
        ld(trif[:], trif_d, [b_const]); ld(trib[:], trib_d, [b_const])
        ld(condt[:], cond_d, [b_const])
        P.dma("pool", lambda e: e.dma_start(out=tmask[:], in_=tmask_d.partition_broadcast(128)), writes=[b_const])
        A(lambda e: e.activation(out=scond[:], in_=condt[:], func=AF.Silu), r=[b_const], w=[b_const])

        def dump(name, ap, bufs):
            if name in dbg_d:
                ld(dbg_d[name], ap, [], q=("sp" if ap.dtype == F32 else "pool"), r=bufs)

        psn = [4]

        pinned = set()

        def ps_next(ctr=[0]):
            while True:
                i = ctr[0] % psn[0]
                ctr[0] += 1
                if i not in pinned:
                    return pst[i], b_ps[i]

        def sumsq_begin():
            ps, bp = ps_next()
            return {"ps": ps, "bp": bp, "n": 0}

        def sumsq_add(S, src_ap, src_bufs, total):
            i = sqc[0] % 2
            sqc[0] += 1
            A(lambda e: e.activation(out=sq[i][:], in_=src_ap, func=AF.Square), r=src_bufs, w=[b_sq[i]])
            first = S["n"] == 0
            last = S["n"] == total - 1
            for h in range(2):
                M(lambda e, h=h: e.matmul(S["ps"][:, h * 512:(h + 1) * 512], lhsT=ones_b[:], rhs=sq[i][:, h * 512:(h + 1) * 512],
                                          start=first, stop=last), r=[b_sq[i], b_const], w=[S["bp"]])
            S["n"] += 1

        def sumsq_finish(S, n_feat):
            A(lambda e: e.activation(out=rstd[:], in_=S["ps"][:], func=AF.Sqrt, scale=1.0 / n_feat, bias=epsb[:]),
              r=[S["bp"], b_const], w=[b_rstd])
            V(lambda e: e.reciprocal(out=rstd[:], in_=rstd[:]), r=[b_rstd], w=[b_rstd])

        rmask = sb("rmask", [128, 4])
        G(lambda e: e.memset(rmask[:], 0.0), w=[b_const])
        for j_ in range(4):
            G(lambda e, j_=j_: e.memset(rmask[32 * j_:32 * j_ + 32, j_:j_ + 1], 1.0), r=[b_const], w=[b_const])
        epsb = sb("epsb", [128, 1])
        G(lambda e: e.memset(epsb[:], EPS), w=[b_const])
        negpi = sb("negpi", [128, 1])
        G(lambda e: e.memset(negpi[:], -math.pi), w=[b_const])

        def proj_fm(wt, col, KC, rhs, rhs_bufs, ps, bp, wb):
            for c in range(KC):
                for h in range(2):
                    M(lambda e, c=c, h=h: e.matmul(ps[:, h * 512:(h + 1) * 512], lhsT=wt[:, c, col:col + 128],
                                                   rhs=rhs[:, c, h * 512:(h + 1) * 512], start=(c == 0), stop=(c == KC - 1)),
                      r=[wb, rhs_bufs[c]], w=[bp])

        tmpn = tmpn_t[:]

        def norm_apply(Ai, Bi):
            for c in range(16):
                V(lambda e, c=c: e.scalar_tensor_tensor(out=tmpn, in0=big[:, c, :], scalar=AB[:, Ai, c:c + 1], in1=rstd[:],
                                                        op0=ALU.mult, op1=ALU.mult), r=[b_big[c], b_AB, b_rstd], w=[b_arena])
                A(lambda e, c=c: e.activation(out=hT[:, c, :], in_=tmpn, func=AF.Identity, bias=AB[:, Bi, c:c + 1], scale=1.0),
                  r=[b_arena, b_AB], w=[b_hT[c]])

        def residual_apply(Gi):
            xt = xt_t[:]
            for c in range(16):
                ld(xt, xs_d[:, c, :], [b_xt], r=[b_big[c]])
                V(lambda e, c=c: e.scalar_tensor_tensor(out=tmpn, in0=big[:, c, :], scalar=AB[:, Gi, c:c + 1], in1=rstd[:],
                                                        op0=ALU.mult, op1=ALU.mult), r=[b_big[c], b_AB, b_rstd], w=[b_arena])
                V(lambda e, c=c: e.tensor_tensor(out=big[:, c, :], in0=tmpn, in1=xt, op=ALU.add), r=[b_arena, b_xt], w=[b_big[c]])

        nt01 = sb("nt01", [128, 8]); hb12 = sb("hb12", [64, 2])
        xv = xT_d.rearrange("(c p) t -> p c t", p=128)
        for c in range(16):
            ld(big[:, c, :], xv[:, c, :], [b_big[c]])

        b_fs_prev = []
        for l in range(n_layers):
            ld(bmodT[:], b_mod_d[l], [b_lw], r=[b_lw]); ld(gvec[:], gvec_d[l], [b_lw])
            ld(hnorm[:], hnorm_d[l], [b_lw]); ld(hcw[:], hcw_d[l], [b_lw]); ld(hcb[:], hcb_d[l], [b_lw])
            ld(hbias[:], hbias_d[l], [b_lw]); ld(fcw[:], fcw_d[l], [b_lw]); ld(fcb[:], fcb_d[l], [b_lw])
            if l == 0:
                ld(hlb[:, 0, :], hlb_d[0], [b_lw]); ld(hlb[:, 1, :], hlb_d[1], [b_lw])
                G(lambda e: e.memset(lbt[:, 0, :], 0.0), w=[b_lbt])
            else:
                V(lambda e: e.tensor_tensor(out=lbt[:, 0, :], in0=hlb[:, 1, :], in1=hlb[:, 0, :], op=ALU.subtract), r=[b_lw, b_lbt], w=[b_lbt])
                A(lambda e: e.activation(out=lbt[:, 0, :], in_=lbt[:, 0, :], func=AF.Sigmoid), r=[b_lbt], w=[b_lbt])
            V(lambda e: e.tensor_scalar(out=lbt[:, 1, :], in0=lbt[:, 0, :], scalar1=-1.0, scalar2=1.0, op0=ALU.mult, op1=ALU.add), r=[b_lbt], w=[b_lbt])
            V(lambda e: e.tensor_scalar(out=lbt[:, 2, :], in0=lbt[:, 0, :], scalar1=1.0, scalar2=-1.0, op0=ALU.mult, op1=ALU.add), r=[b_lbt], w=[b_lbt])
            V(lambda e: e.tensor_scalar(out=hcwc[:, :, 0:1], in0=hcw[:, :, 0:1], scalar1=cfg[:, 0:1], scalar2=None, op0=ALU.mult), r=[b_lw, b_const], w=[b_lw])
            V(lambda e: e.tensor_scalar(out=hcwc[:, :, 1:2], in0=hcw[:, :, 2:3], scalar1=cfg[:, 0:1], scalar2=None, op0=ALU.mult), r=[b_lw, b_const], w=[b_lw])
            for k in (0, 1, 2, 6, 7, 8):
                V(lambda e, k=k: e.tensor_scalar(out=fcw[:, :, k:k + 1], in0=fcw[:, :, k:k + 1], scalar1=cfg[:, 0:1], scalar2=None, op0=ALU.mult), r=[b_lw, b_const], w=[b_lw])

            psm, bpm = ps_next()
            first = True
            for ng in range(6):
                for c in range(16):
                    wt, wb = wload(w_mod_d[l][c * 128:(c + 1) * 128, ng * 2048:(ng + 1) * 2048], [128, 2048])
                    for jj in range(16):
                        j = ng * 16 + jj
                        M(lambda e, jj=jj, j=j, c=c, wt=wt, f=first: e.matmul(psm[:, j:j + 1], lhsT=wt[:, jj * 128:(jj + 1) * 128], rhs=scond[:, c:c + 1],
                                                                              start=f, stop=(ng == 5 and c == 15 and jj == 15), skip_group_check=True),
                          r=[wb, b_const], w=[bpm])
                        first = False
            V(lambda e: e.tensor_tensor(out=modT[:], in0=psm[:, 0:96], in1=bmodT[:], op=ALU.add), r=[bpm, b_lw], w=[b_mod])
            for s in range(2):
                o = 48 * s
                V(lambda e, s=s, o=o: e.scalar_tensor_tensor(out=AB[:, 3 * s + 0, :], in0=modT[:, o + 16:o + 32], scalar=1.0, in1=gvec[:, 2 * s, :],
                                                             op0=ALU.add, op1=ALU.mult), r=[b_mod, b_lw], w=[b_AB])
                V(lambda e, s=s, o=o: e.tensor_copy(out=AB[:, 3 * s + 1, :], in_=modT[:, o:o + 16]), r=[b_mod], w=[b_AB])
                V(lambda e, s=s, o=o: e.tensor_tensor(out=AB[:, 3 * s + 2, :], in0=modT[:, o + 32:o + 48], in1=gvec[:, 2 * s + 1, :], op=ALU.mult), r=[b_mod, b_lw], w=[b_AB])
            dump("modT%d" % l, modT[:], [b_mod])

            S = sumsq_begin()
            for c in range(16):
                sumsq_add(S, big[:, c, :], [b_big[c]], 16)
            sumsq_finish(S, D)
            norm_apply(0, 1)
            for c in range(16):
                ld(xs_d[:, c, :], big[:, c, :], [], r=[b_big[c]])
            b_xs = b_big
            dump("hT%d" % l, hT[:, 0, :], [b_hT[0]])
            w_in = w_in_d[l]


            def alias(new, old):
                for nb in new:
                    for ob in old:
                        if ob.w is not None:
                            k, v = ob.w
                            if nb.r.get(k, 0) < v:
                                nb.r[k] = v
                        for k, v in ob.r.items():
                            if nb.r.get(k, 0) < v:
                                nb.r[k] = v

            def wchunk(w2d, col, KC=16):
                return wload(kview(w2d, col, col + 128, KC), [128, KC, 128])

            def psbf(ps):
                return ps[:, 0:512].bitcast(BF16)

            HB = {k: Buf("hy_" + k) for k in ("vT", "x1T", "x2T", "Kp", "t12", "t34", "hwj", "win", "hfp", "hbp", "hp", "hm", "ab",
                                              "ztok", "zbf", "P", "Q", "rinv", "decb")}
            alias(list(HB.values()), b_big)
            alias([b_MC], [b_U, b_dg[0], b_dg[1], b_sa, b_gT[0], b_gT[1]] + b_mg)
            alias(b_ob, b_fs_prev)
            vT = cf32(0, 1024); x1T = cf32(1024, 1024); x2T = cf32(2048, 1024)
            Kp = cf32(3072, 4096).rearrange("p (f c n) -> p f c n", c=2, n=256)
            t12 = cf32(7168, 1024); t34 = cf32(8192, 1024)
            hwj = cf32(9216, 512); win = cf32(9728, 256); hfp = cf32(9984, 256); hbp = cf32(10240, 256)
            hp = cbf(10496, 2048).rearrange("p (a b) -> p a b", b=256)
            hm = cbf(11520, 2048).rearrange("p (a b) -> p a b", b=256)
            ab = cbf(12544, 2048).rearrange("p (a b) -> p a b", b=256)
            ztok = cbf(13568, 1024).rearrange("p (a b) -> p a b", b=128)
            zbf = cbf(14080, 1024)
            Pq = cbf(14592, 1024).rearrange("p (a b) -> p a b", b=128)
            Qq = cbf(15104, 1024).rearrange("p (a b) -> p a b", b=128)
            rinv = cf32(15616, 256); decb = cf32(15872, 256)
            xtb = xt_t[:].bitcast(BF16)
            tnb = tmpn_t[:].bitcast(BF16)
            h1T = xtb[0:64, 0:T]; h2T = xtb[0:64, T:2 * T]
            featsT = tnb[0:33, 0:T]; w1b = tnb[0:33, T:T + 64]; w2b = tnb[0:64, T + 64:T + 128]
            w3c = tnb[0:64, T + 128:T + 640]
            ld(MC, MC_d.rearrange("(a p) b -> p a b", p=128), [b_MC], q="pool")
            ld(MS, MS_d.rearrange("(a p) b -> p a b", p=128), [b_MC], q="pool")
            ld(featsT, feats_d, [b_arena], q="pool")
            ld(w1b, hw1_d[l], [b_arena], q="pool"); ld(w2b, hw2_d[l], [b_arena], q="pool")
            ld(hb12[:, 0:1], hb1_d[l], [b_lw], r=[b_lw]); ld(hb12[:, 1:2], hb2_d[l], [b_lw], r=[b_lw])
            V(lambda e: e.tensor_scalar(out=nt01[:], in0=rowt[:, :, 0], scalar1=-1.0, scalar2=None, op0=ALU.mult), r=[b_const], w=[b_lw])
            pre = t12[0:64, :]
            for li, (wl, src, dst, KK) in enumerate(((w1b, featsT, h1T, 33), (w2b, h1T, h2T, 64))):
                ps, bp = ps_next()
                for h in range(2):
                    M(lambda e, h=h, ps=ps, wl=wl, src=src: e.matmul(ps[0:64, h * 512:(h + 1) * 512], lhsT=wl, rhs=src[:, h * 512:(h + 1) * 512], start=True, stop=True),
                      r=[b_arena, b_xt], w=[bp])
                V(lambda e, ps=ps, li=li: e.tensor_scalar(out=pre, in0=ps[0:64, :], scalar1=hb12[:, li:li + 1], scalar2=None, op0=ALU.add), r=[bp, b_lw], w=[HB["t12"]])
                mA = t34[0:64, :]; mB = cf32(3072, 1024)[0:64, :]
                V(lambda e: e.tensor_scalar(out=mA, in0=pre, scalar1=-math.pi, scalar2=2 * math.pi, op0=ALU.is_lt, op1=ALU.mult), r=[HB["t12"]], w=[HB["t34"]])
                V(lambda e: e.tensor_scalar(out=mB, in0=pre, scalar1=math.pi, scalar2=-2 * math.pi, op0=ALU.is_gt, op1=ALU.mult), r=[HB["t12"]], w=[HB["Kp"]])
                V(lambda e: e.tensor_tensor(out=pre, in0=pre, in1=mA, op=ALU.add), r=[HB["t12"], HB["t34"]], w=[HB["t12"]])
                V(lambda e: e.tensor_tensor(out=pre, in0=pre, in1=mB, op=ALU.add), r=[HB["t12"], HB["Kp"]], w=[HB["t12"]])
                A(lambda e, dst=dst: e.activation(out=dst, in_=pre, func=AF.Sin), r=[HB["t12"]], w=[b_xt])
            w3v = hw3_d[l].rearrange("k (g c) -> k g c", c=1024)
            for cc in range(8):
                for typ, (dst, bdst) in enumerate(((vT, HB["vT"]), (x1T, HB["x1T"]), (x2T, HB["x2T"]))):
                    ci = typ * 8 + cc
                    wt, wb = wchunk(w_in, 5120 + typ * 1024 + cc * 128)
                    ps, bp = ps_next()
                    proj_fm(wt, 0, 16, hT, b_hT, ps, bp, wb)
                    A(lambda e, ps=ps, dst=dst, ci=ci: e.activation(out=dst, in_=ps[:], func=AF.Identity, scale=hcw[:, ci, 1:2], bias=hcb[:, ci:ci + 1]),
                      r=[bp, b_lw], w=[bdst])
                    d3 = dst.rearrange("p (s i) -> p s i", i=256)
                    p3 = ps[:].rearrange("p (s i) -> p s i", i=256)
                    V(lambda e, d3=d3, p3=p3, ci=ci: e.scalar_tensor_tensor(out=d3[:, :, 1:256], in0=p3[:, :, 0:255], scalar=hcw[:, ci, 0:1], in1=d3[:, :, 1:256],
                                                                           op0=ALU.mult, op1=ALU.add), r=[bp, b_lw, bdst], w=[bdst])
                    V(lambda e, d3=d3, p3=p3, ci=ci: e.scalar_tensor_tensor(out=d3[:, 1:4, 0:1], in0=p3[:, 0:3, 255:256], scalar=hcwc[:, ci, 0:1], in1=d3[:, 1:4, 0:1],
                                                                           op0=ALU.mult, op1=ALU.add), r=[bp, b_lw, bdst], w=[bdst])
                    V(lambda e, d3=d3, p3=p3, ci=ci: e.scalar_tensor_tensor(out=d3[:, :, 0:255], in0=p3[:, :, 1:256], scalar=hcw[:, ci, 2:3], in1=d3[:, :, 0:255],
                                                                           op0=ALU.mult, op1=ALU.add), r=[bp, b_lw, bdst], w=[bdst])
                    V(lambda e, d3=d3, p3=p3, ci=ci: e.scalar_tensor_tensor(out=d3[:, 0:3, 255:256], in0=p3[:, 1:4, 0:1], scalar=hcwc[:, ci, 1:2], in1=d3[:, 0:3, 255:256],
                                                                           op0=ALU.mult, op1=ALU.add), r=[bp, b_lw, bdst], w=[bdst])
                ld(w3c.rearrange("k (g c) -> k g c", c=128), w3v[:, :, cc * 128:(cc + 1) * 128], [b_arena], q="pool", r=[b_arena])
                ld(decb.rearrange("p (o c) -> p o c", c=128), hdec_d[l][:, cc * 128:(cc + 1) * 128].partition_broadcast(128), [HB["decb"]])
                A(lambda e: e.activation(out=decb, in_=decb, func=AF.Abs), r=[HB["decb"]], w=[HB["decb"]])
                hw4 = hwj.rearrange("p (o f c) -> p o f c", f=2, c=128)
                win4 = win.rearrange("p (o c) -> p o c", c=128).unsqueeze(2).to_broadcast([128, 2, 2, 128])
                hf3 = hw4[:, :, 0, :]; hb3 = hw4[:, :, 1, :]
                hfp3 = hfp.rearrange("p (o c) -> p o c", c=128); hbp3 = hbp.rearrange("p (o c) -> p o c", c=128)
                for jt in range(8):
                    ps, bp = ps_next()
                    M(lambda e, ps=ps, jt=jt: e.matmul(ps[:, 0:512], lhsT=h2T[:, jt * 128:(jt + 1) * 128], rhs=w3c, start=True, stop=True), r=[b_xt, b_arena], w=[bp])
                    A(lambda e, jt=jt: e.activation(out=win, in_=decb, func=AF.Exp, scale=nt01[:, jt:jt + 1]), r=[HB["decb"], b_lw], w=[HB["win"]])
                    V(lambda e, ps=ps: e.tensor_tensor(out=hw4, in0=ps[:, 0:512].rearrange("p (o f c) -> p o f c", f=2, c=128), in1=win4, op=ALU.mult),
                      r=[bp, HB["win"]], w=[HB["hwj"]])
                    V(lambda e, jt=jt: e.scalar_tensor_tensor(out=hfp3, in0=hb3, scalar=rowt[:, jt, 1:2], in1=hf3, op0=ALU.mult, op1=ALU.add), r=[HB["hwj"], b_const], w=[HB["hfp"]])
                    V(lambda e, jt=jt: e.tensor_scalar(out=hbp3, in0=hb3, scalar1=rowt[:, jt, 2:3], scalar2=None, op0=ALU.mult), r=[HB["hwj"], b_const], w=[HB["hbp"]])
                    V(lambda e, jt=jt: e.tensor_tensor(out=hp[:, jt, :], in0=hfp, in1=hbp, op=ALU.add), r=[HB["hfp"], HB["hbp"]], w=[HB["hp"]])
                    V(lambda e, jt=jt: e.tensor_tensor(out=hm[:, jt, :], in0=hfp, in1=hbp, op=ALU.subtract), r=[HB["hfp"], HB["hbp"]], w=[HB["hm"]])
                    A(lambda e: e.activation(out=hfp, in_=hfp, func=AF.Abs), r=[HB["hfp"]], w=[HB["hfp"]])
                    A(lambda e: e.activation(out=hbp, in_=hbp, func=AF.Abs), r=[HB["hbp"]], w=[HB["hbp"]])
                    V(lambda e, jt=jt: e.tensor_tensor(out=ab[:, jt, :], in0=hfp, in1=hbp, op=ALU.add), r=[HB["hfp"], HB["hbp"]], w=[HB["ab"]])
                ps, bp = ps_next()
                for jt in range(8):
                    M(lambda e, ps=ps, jt=jt: e.matmul(ps[:, 0:256], lhsT=ones_b[:], rhs=ab[:, jt, :], start=(jt == 0), stop=(jt == 7)), r=[HB["ab"], b_const], w=[bp])
                V(lambda e, ps=ps: e.tensor_scalar(out=rinv, in0=ps[:, 0:256], scalar1=cfg[:, 2:3], scalar2=EPS, op0=ALU.mult, op1=ALU.add), r=[bp, b_const], w=[HB["rinv"]])
                V(lambda e: e.reciprocal(out=rinv, in_=rinv), r=[HB["rinv"]], w=[HB["rinv"]])
                FCv = FC_d.rearrange("(a p) f -> p a f", p=128); FSv = FS_d.rearrange("(a p) f -> p a f", p=128)
                for hh in range(2):
                    FCt, fcbuf = wload(FCv[:, :, hh * 512:(hh + 1) * 512], [128, 8, 512])
                    FSt, fsbuf = wload(FSv[:, :, hh * 512:(hh + 1) * 512], [128, 8, 512])
                    for ftl in range(4):
                        ft = hh * 4 + ftl
                        ps, bp = ps_next()
                        for jt in range(8):
                            M(lambda e, ps=ps, jt=jt, ftl=ftl, FCt=FCt: e.matmul(ps[:, 0:256], lhsT=FCt[:, jt, ftl * 128:(ftl + 1) * 128], rhs=hp[:, jt, :], start=(jt == 0), stop=(jt == 7)),
                              r=[fcbuf, HB["hp"]], w=[bp])
                        for jt in range(8):
                            M(lambda e, ps=ps, jt=jt, ftl=ftl, FSt=FSt: e.matmul(ps[:, 256:512], lhsT=FSt[:, jt, ftl * 128:(ftl + 1) * 128], rhs=hm[:, jt, :], start=False, stop=(jt == 7), skip_group_check=True),
                              r=[fsbuf, HB["hm"]], w=[bp])
                        for cs in range(2):
                            V(lambda e, ps=ps, ft=ft, cs=cs: e.scalar_tensor_tensor(out=Kp[:, ft, cs, :], in0=ps[:, cs * 256:(cs + 1) * 256], scalar=rowt[:, ft, 3:4], in1=rinv,
                                                                                   op0=ALU.mult, op1=ALU.mult), r=[bp, b_const, HB["rinv"]], w=[HB["Kp"]])
                for o in range(2):
                    zin, bzin = (vT, HB["vT"]) if o == 0 else (x1T, HB["x1T"])
                    gate, bgate = (x1T, HB["x1T"]) if o == 0 else (x2T, HB["x2T"])
                    A(lambda e, zin=zin: e.activation(out=zbf, in_=zin, func=AF.Identity), r=[bzin], w=[HB["zbf"]])
                    ps, bp = ps_next()
                    pb = psbf(ps)
                    for tt in range(8):
                        M(lambda e, pb=pb, tt=tt: e.transpose(out=pb[:, tt * 128:(tt + 1) * 128], in_=zbf[:, tt * 128:(tt + 1) * 128], identity=ident_b[:]), r=[HB["zbf"], b_const], w=[bp])
                    A(lambda e, pb=pb: e.activation(out=ztok.rearrange("p a b -> p (a b)"), in_=pb, func=AF.Identity), r=[bp], w=[HB["ztok"]])
                    psc_, bpc_ = ps_next(); pss_, bps_ = ps_next()
                    for (pz, bz, Mx) in ((psc_, bpc_, MC), (pss_, bps_, MS)):
                        for ft in range(8):
                            for tt in range(8):
                                M(lambda e, pz=pz, ft=ft, tt=tt, Mx=Mx: e.matmul(pz[:, ft * 128:(ft + 1) * 128], lhsT=Mx[:, tt, ft * 128:(ft + 1) * 128], rhs=ztok[:, tt, :], start=(tt == 0), stop=(tt == 7)),
                                  r=[b_MC, HB["ztok"]], w=[bz])
                    for hh in range(2):
                        Kc = Kp[:, hh * 4:hh * 4 + 4, 0, o * 128:(o + 1) * 128]; Ks = Kp[:, hh * 4:hh * 4 + 4, 1, o * 128:(o + 1) * 128]
                        Zc = psc_[:, hh * 512:(hh + 1) * 512].rearrange("p (a b) -> p a b", b=128); Zs = pss_[:, hh * 512:(hh + 1) * 512].rearrange("p (a b) -> p a b", b=128)
                        ta = t12[:, 0:512].rearrange("p (a b) -> p a b", b=128); tb = t12[:, 512:1024].rearrange("p (a b) -> p a b", b=128)
                        tc = t34[:, 0:512].rearrange("p (a b) -> p a b", b=128); td = t34[:, 512:1024].rearrange("p (a b) -> p a b", b=128)
                        V(lambda e, Zc=Zc, Kc=Kc, ta=ta: e.tensor_tensor(out=ta, in0=Zc, in1=Kc, op=ALU.mult), r=[bpc_, HB["Kp"]], w=[HB["t12"]])
                        V(lambda e, Zs=Zs, Ks=Ks, tb=tb: e.tensor_tensor(out=tb, in0=Zs, in1=Ks, op=ALU.mult), r=[bps_, HB["Kp"]], w=[HB["t12"]])
                        V(lambda e, ta=ta, tb=tb, hh=hh: e.tensor_tensor(out=Pq[:, hh * 4:hh * 4 + 4, :], in0=ta, in1=tb, op=ALU.subtract), r=[HB["t12"]], w=[HB["P"]])
                        V(lambda e, Zc=Zc, Ks=Ks, tc=tc: e.tensor_tensor(out=tc, in0=Zc, in1=Ks, op=ALU.mult), r=[bpc_, HB["Kp"]], w=[HB["t34"]])
                        V(lambda e, Zs=Zs, Kc=Kc, td=td: e.tensor_tensor(out=td, in0=Zs, in1=Kc, op=ALU.mult), r=[bps_, HB["Kp"]], w=[HB["t34"]])
                        V(lambda e, tc=tc, td=td, hh=hh: e.tensor_tensor(out=Qq[:, hh * 4:hh * 4 + 4, :], in0=tc, in1=td, op=ALU.add), r=[HB["t34"]], w=[HB["Q"]])
                    psy, bpy = ps_next()
                    for h in range(2):
                        for ft in range(8):
                            M(lambda e, psy=psy, h=h, ft=ft: e.matmul(psy[:, h * 512:(h + 1) * 512], lhsT=Pq[:, ft, :], rhs=MC[:, ft, h * 512:(h + 1) * 512], start=(ft == 0), stop=False),
                              r=[HB["P"], b_MC], w=[bpy])
                        for ft in range(8):
                            M(lambda e, psy=psy, h=h, ft=ft: e.matmul(psy[:, h * 512:(h + 1) * 512], lhsT=Qq[:, ft, :], rhs=MS[:, ft, h * 512:(h + 1) * 512], start=False, stop=(ft == 7)),
                              r=[HB["Q"], b_MC], w=[bpy])
                    V(lambda e, psy=psy, zin=zin, o=o, cc=cc: e.scalar_tensor_tensor(out=t12, in0=zin, scalar=hbias[:, o, cc:cc + 1], in1=psy[:], op0=ALU.mult, op1=ALU.add),
                      r=[bzin, b_lw, bpy], w=[HB["t12"]])
                    if o == 0:
                        V(lambda e: e.tensor_tensor(out=x1T, in0=t12, in1=x1T, op=ALU.mult), r=[HB["t12"], HB["x1T"]], w=[HB["x1T"]])
                    else:
                        V(lambda e, cc=cc: e.tensor_tensor(out=obT[:, cc, :], in0=t12, in1=x2T, op=ALU.mult), r=[HB["t12"], HB["x2T"]], w=[b_ob[cc]])
            dump("obT%d" % l, obT[:, 0, :], [b_ob[0]])
            dump("obT7_%d" % l, obT[:, 7, :], [b_ob[7]])

            GB = {k: Buf("hg_" + k) for k in ("qT", "sgT", "sig", "logf", "kk", "tmpE", "kblT", "Sall", "Sst", "attT", "t1")}
            GP = [{k: Buf("hg%d_%s" % (p_, k)) for k in ("qb", "qb32", "kb", "kbl", "kbl3", "edec", "vtok")} for p_ in range(2)]
            b_oa = [Buf("oa%d" % i) for i in range(8)]
            allg = list(GB.values()) + [b for d_ in GP for b in d_.values()] + b_oa
            alias(allg, list(HB.values()) + [b_MC, b_xt])
            qT = cf32(0, 1024); sgT = cf32(1024, 1024); sig = cf32(2048, 1024); logf = cf32(3072, 1024); kk = cf32(4096, 1024); tmpE = cf32(5120, 1024)
            kblT = cbf(7680, 1024)
            Sall = cbf(8704, 4096).rearrange("p (a b) -> p a b", b=128)
            Sst = cf32(10752, 128); attT = cbf(10944, 128); t1 = cf32(11008, 1024)
            oaT = cbf(12288, 8192).rearrange("p (a b) -> p a b", b=T)
            xtb2 = xt_t[:].bitcast(BF16)
            QB = [cbf(6656, 1024), ar2[:, 0:1024]]
            KB = [cbf(7168, 1024), ar2[:, 1024:2048]]
            KBL = [cbf(8192, 1024).rearrange("p (a b) -> p a b", b=128), ar2[:, 2048:3072].rearrange("p (a b) -> p a b", b=128)]
            KBL3 = [xtb2[:, 0:1024].rearrange("p (a b) -> p a b", b=128), xtb2[:, 1024:2048].rearrange("p (a b) -> p a b", b=128)]
            EDEC = [cf32(10880, 32), ar2[:, 3072:3136].bitcast(F32)]
            QB32 = [cf32(8704, 1024), cf32(9728, 1024)]
            RS = 16
            Sring = ar2[:, 5120:5120 + RS * 256].bitcast(F32).rearrange("p (a b) -> p a b", b=128)
            b_sr = [Buf("sr%d" % i) for i in range(RS)]
            Sbf = ar2[:, 9216:9216 + RS * 128].rearrange("p (a b) -> p a b", b=128)
            b_sb = [Buf("sb%d" % i) for i in range(RS)]
            alias(b_sr + b_sb, list(HB.values()) + [b_MC])
            VTOK = [cbf(6144, 1024).rearrange("p (a b) -> p a b", b=128), ar2[:, 4096:5120].rearrange("p (a b) -> p a b", b=128)]
            VBLK = [cbf(8704, 4096).rearrange("p (a j b) -> p a j b", j=4, b=128), ar2[:, 11264:15360].rearrange("p (a j b) -> p a j b", j=4, b=128)]
            b_vb = [Buf("vblk0"), Buf("vblk1")]
            alias(b_vb, list(HB.values()) + [b_MC])
            psn[0] = 1
            psO, bpO = pst[3], b_ps[3]
            b_psd = [Buf("psd0"), Buf("psd1")]
            alias(b_psd, [b_ps[2]])
            dsc = [0]
            iters = [(h_, d_) for h_ in range(8) for d_ in range(2)]

            def prep_thunks(it):
                h, dr = iters[it]
                p_ = it % 2
                hp_ = h % 2
                gp = GP[p_]
                qb, kb, kbl, kbl3, edec = QB[p_], KB[p_], KBL[p_], KBL3[p_], EDEC[p_]
                qb32 = QB32[p_]
                vtok = VTOK[hp_]; bvt = GP[hp_]["vtok"]
                idx = dr * 8 + h
                th = []
                if dr == 0:
                    def t_q():
                        wt, wb = wchunk(w_in, h * 128)
                        ps, bp = ps_next(); proj_fm(wt, 0, 16, hT, b_hT, ps, bp, wb)
                        A(lambda e: e.activation(out=qT, in_=ps[:], func=AF.Silu), r=[bp], w=[GB["qT"]])
                    th.append(t_q)

                    def t_v():
                        wt, wb = wchunk(w_in, (24 + h) * 128)
                        ps, bp = ps_next()
                        for tt in range(8):
                            for c in range(16):
                                M(lambda e, tt=tt, c=c: e.matmul(ps[:, tt * 128:(tt + 1) * 128], lhsT=hT[:, c, tt * 128:(tt + 1) * 128], rhs=wt[:, c, :], start=(c == 0), stop=(c == 15)),
                                  r=[wb, b_hT[c]], w=[bp])
                        A(lambda e: e.activation(out=vtok.rearrange("p a b -> p (a b)"), in_=ps[:], func=AF.Identity), r=[bp], w=[bvt])
                        vb = VBLK[hp_]
                        for j in range(4):
                            G(lambda e, j=j: e.tensor_scalar(out=vb[:, :, j, :], in0=vtok, scalar1=rmask[:, j:j + 1], scalar2=1.0, op0=ALU.mult, op1=ALU.mult), r=[bvt, b_const], w=[b_vb[hp_]])
                    th.append(t_v)

                def t_f():
                    wt, wb = wchunk(w_in, (8 + 8 * dr + h) * 128)
                    ps, bp = ps_next(); proj_fm(wt, 0, 16, hT, b_hT, ps, bp, wb)
                    A(lambda e: e.activation(out=sig, in_=ps[:], func=AF.Sigmoid), r=[bp], w=[GB["sig"]])
                th.append(t_f)

                def t_ln():
                    A(lambda e: e.activation(out=logf, in_=sig, func=AF.Ln, scale=lbt[:, 1, idx:idx + 1], bias=lbt[:, 0, idx:idx + 1]), r=[GB["sig"], b_lbt], w=[GB["logf"]])
                    V(lambda e: e.tensor_scalar(out=kk, in0=sig, scalar1=lbt[:, 2, idx:idx + 1], scalar2=lbt[:, 1, idx:idx + 1], op0=ALU.mult, op1=ALU.add),
                      r=[GB["sig"], b_lbt], w=[GB["kk"]])
                th.append(t_ln)
                if dr == 0:
                    bb, bbb, anc = sig, GB["sig"], 31
                else:
                    bb, bbb, anc = logf, GB["logf"], 0
                bb3 = bb.rearrange("p (n i) -> p n i", i=32)

                def t_scan():
                    V(lambda e: e.tensor_tensor_scan(out=sig, data0=tmask[:, 0, :], data1=logf, initial=0.0, op0=ALU.mult, op1=ALU.add),
                      r=[GB["logf"], b_const], w=[GB["sig"]])
                    if dr == 1:
                        V(lambda e: e.scalar_tensor_tensor(out=tmpE, in0=sig, scalar=-1.0, in1=logf, op0=ALU.mult, op1=ALU.add), r=[GB["sig"], GB["logf"]], w=[GB["tmpE"]])
                        s3 = sig.rearrange("p (n i) -> p n i", i=32)
                        V(lambda e: e.tensor_tensor(out=logf.rearrange("p (n i) -> p n i", i=32), in0=tmpE.rearrange("p (n i) -> p n i", i=32),
                                                    in1=s3[:, :, 31:32].to_broadcast([128, 32, 32]), op=ALU.add), r=[GB["tmpE"], GB["sig"]], w=[GB["logf"]])
                th.append(t_scan)

                def t_q2():
                    A(lambda e: e.activation(out=edec.rearrange("p (n i) -> p n i", i=1), in_=bb3[:, :, anc:anc + 1], func=AF.Exp), r=[bbb], w=[gp["edec"]])
                    A(lambda e: e.activation(out=tmpE, in_=bb, func=AF.Exp), r=[bbb], w=[GB["tmpE"]])
                    V(lambda e: e.tensor_tensor(out=qb, in0=qT, in1=tmpE, op=ALU.mult), r=[GB["qT"], GB["tmpE"]], w=[gp["qb"]])
                th.append(t_q2)

                def t_k2():
                    A(lambda e: e.activation(out=tmpE, in_=bb, func=AF.Exp, scale=-1.0), r=[bbb], w=[GB["tmpE"]])
                    V(lambda e: e.tensor_tensor(out=kb, in0=kk, in1=tmpE, op=ALU.mult), r=[GB["kk"], GB["tmpE"]], w=[gp["kb"]])
                th.append(t_k2)

                def t_kl():
                    V(lambda e: e.tensor_tensor(out=tmpE.rearrange("p (n i) -> p n i", i=32), in0=bb3[:, :, anc:anc + 1].to_broadcast([128, 32, 32]), in1=bb3, op=ALU.subtract),
                      r=[bbb], w=[GB["tmpE"]])
                    A(lambda e: e.activation(out=tmpE, in_=tmpE, func=AF.Exp), r=[GB["tmpE"]], w=[GB["tmpE"]])
                    V(lambda e: e.tensor_tensor(out=kblT, in0=kk, in1=tmpE, op=ALU.mult), r=[GB["kk"], GB["tmpE"]], w=[GB["kblT"]])
                th.append(t_kl)

                def t_tr():
                    ps, bp = ps_next(); pb = psbf(ps)
                    for tt in range(8):
                        M(lambda e, tt=tt: e.transpose(out=pb[:, tt * 128:(tt + 1) * 128], in_=kblT[:, tt * 128:(tt + 1) * 128], identity=ident_b[:]), r=[GB["kblT"], b_const], w=[bp])
                    A(lambda e: e.activation(out=kbl.rearrange("p a b -> p (a b)"), in_=pb, func=AF.Identity), r=[bp], w=[gp["kbl"]])
                th.append(t_tr)
                return th

            psd_of = {}

            def dS_emit(it, tt):
                h, dr = iters[it]
                p_ = it % 2
                gp = GP[p_]
                qb, kb, kbl, kbl3, edec = QB[p_], KB[p_], KBL[p_], KBL3[p_], EDEC[p_]
                qb32 = QB32[p_]
                vtok = VTOK[h % 2]; bvt = GP[h % 2]["vtok"]
                di = dsc[0] % 2
                dsc[0] += 1
                psd, bpd = pst[2][:, di * 512:(di + 1) * 512], b_psd[di]
                psd_of[(it, tt)] = (psd, bpd)
                vb = VBLK[h % 2]
                M(lambda e: e.matmul(psd[:, 0:512], lhsT=kbl[:, tt, :], rhs=vb[:, tt, :, :].rearrange("p j b -> p (j b)"), start=True, stop=True),
                  r=[gp["kbl"], b_vb[h % 2]], w=[bpd], acc=False)

            def tile_rest(it, tt):
                h, dr = iters[it]
                p_ = it % 2
                gp = GP[p_]
                qb, kb, kbl, kbl3, edec = QB[p_], KB[p_], KBL[p_], KBL3[p_], EDEC[p_]
                vtok = VTOK[h % 2]; bvt = GP[h % 2]["vtok"]
                tri = trif if dr == 0 else trib
                psd, bpd = psd_of.pop((it, tt))
                chunks = range(4) if dr == 0 else range(3, -1, -1)
                for j in chunks:
                    n = tt * 4 + j
                    step = n if dr == 0 else 31 - n
                    ec, en = step % RS, (step + 1) % RS
                    A(lambda e, ec=ec: e.activation(out=Sbf[:, ec, :], in_=Sring[:, ec, :], func=AF.Identity), r=[b_sr[ec]], w=[b_sb[ec]])
                    V(lambda e, n=n, j=j, ec=ec, en=en: e.scalar_tensor_tensor(out=Sring[:, en, :], in0=Sring[:, ec, :], scalar=edec[:, n:n + 1], in1=psd[:, j * 128:(j + 1) * 128],
                                                                              op0=ALU.mult, op1=ALU.add), r=[b_sr[ec], gp["edec"], bpd], w=[b_sr[en]])
                    segend = (n % 8 == 7) if dr == 0 else (n % 8 == 0)
                    if segend:
                        ld(st_d[l, n // 8, dr, h], Sring[:, en, :], [], r=[b_sr[en]])
                        last = (n == 31) if dr == 0 else (n == 0)
                        if not last:
                            V(lambda e, en=en: e.tensor_scalar(out=Sring[:, en, :], in0=Sring[:, en, :], scalar1=cfg[:, 0:1], scalar2=None, op0=ALU.mult), r=[b_sr[en], b_const], w=[b_sr[en]])
                psa, bpa = pst[1], b_ps[1]
                M(lambda e: e.matmul(psa[:, 0:128], lhsT=kb[:, tt * 128:(tt + 1) * 128], rhs=qb[:, tt * 128:(tt + 1) * 128], start=True, stop=True),
                  r=[gp["kb"], gp["qb"]], w=[bpa], acc=False)
                V(lambda e: e.tensor_tensor(out=attT, in0=psa[:, 0:128], in1=tri[:], op=ALU.mult), r=[bpa, b_const], w=[GB["attT"]])
                first = (dr == 0 and tt in (0, 4))
                M(lambda e: e.matmul(psO[:, tt * 128:(tt + 1) * 128], lhsT=vtok[:, tt, :], rhs=attT, start=first, stop=False, skip_group_check=True),
                  r=[bvt, GB["attT"]], w=[bpO])
                for j in range(4):
                    n = tt * 4 + j
                    step = n if dr == 0 else 31 - n
                    ec = step % RS
                    M(lambda e, n=n, j=j, ec=ec: e.matmul(psO[:, n * 32:(n + 1) * 32], lhsT=Sbf[:, ec, :], rhs=qb[:, n * 32:(n + 1) * 32], start=False, stop=(dr == 1 and tt == 0 and j == 3), skip_group_check=True),
                      r=[b_sb[ec], gp["qb"]], w=[bpO])

            def head_finish(h):
                wt, wb = wchunk(w_in, (32 + h) * 128)
                ps, bp = ps_next(); proj_fm(wt, 0, 16, hT, b_hT, ps, bp, wb)
                A(lambda e: e.activation(out=sgT, in_=ps[:], func=AF.Silu), r=[bp], w=[GB["sgT"]])
                i = sqc[0] % 2; sqc[0] += 1
                A(lambda e: e.activation(out=sq[i][:], in_=psO[:], func=AF.Square), r=[bpO], w=[b_sq[i]])
                pss, bpss = ps_next()
                for hf_ in range(2):
                    M(lambda e, hf_=hf_: e.matmul(pss[:, hf_ * 512:(hf_ + 1) * 512], lhsT=ones_b[:], rhs=sq[i][:, hf_ * 512:(hf_ + 1) * 512], start=True, stop=True),
                      r=[b_sq[i], b_const], w=[bpss], acc=False)
                A(lambda e: e.activation(out=rstd[:], in_=pss[:], func=AF.Sqrt, scale=1.0 / 128, bias=epsb[:]), r=[bpss, b_const], w=[b_rstd])
                V(lambda e: e.reciprocal(out=rstd[:], in_=rstd[:]), r=[b_rstd], w=[b_rstd])
                V(lambda e: e.scalar_tensor_tensor(out=t1, in0=psO[:], scalar=hnorm[:, 0:1], in1=rstd[:], op0=ALU.mult, op1=ALU.mult), r=[bpO, b_lw, b_rstd], w=[GB["t1"]])
                V(lambda e: e.tensor_tensor(out=oaT[:, h, :], in0=t1, in1=sgT, op=ALU.mult), r=[GB["t1"], GB["sgT"]], w=[b_oa[h]])

            for f_ in prep_thunks(0):
                f_()
            for it in range(16):
                h, dr = iters[it]
                nxt = prep_thunks(it + 1) if it + 1 < 16 else []
                ld(Sring[:, 0, :], s0_d[l, dr, h], [b_sr[0]])
                tiles = list(range(8)) if dr == 0 else list(range(7, -1, -1))
                per = (len(nxt) + 7) // 8
                dS_emit(it, tiles[0])
                for k_, tt in enumerate(tiles):
                    if k_ + 1 < 8:
                        dS_emit(it, tiles[k_ + 1])
                    tile_rest(it, tt)
                    for f_ in nxt[k_ * per:(k_ + 1) * per]:
                        f_()
                for f_ in nxt[8 * per:]:
                    f_()
                if dr == 1:
                    head_finish(h)
            psn[0] = 4
            alias([b_ps[2]], b_psd)
            dump("oaT%d" % l, oaT[:, 0, :], [b_oa[0]])
            dump("oaT7_%d" % l, oaT[:, 7, :], [b_oa[7]])

            alias(b_mg, [b_MC] + allg + b_sr + b_sb + b_vb)
            alias([b_arena, b_xt], allg)
            wba = wba_d[l]; wbb = wbb_d[l]
            for j in range(16):
                for br, (wbr2d, oT, boT, gbase) in enumerate(((wba, oaT, b_oa, 8192), (wbb, obT, b_ob, 10240))):
                    wg, wgb = wchunk(w_in, gbase + j * 128)
                    psg, bpg = ps_next(); proj_fm(wg, 0, 16, hT, b_hT, psg, bpg, wgb)
                    A(lambda e, psg=psg: e.activation(out=tmpn, in_=psg[:], func=AF.Sigmoid), r=[bpg], w=[b_arena])
                    wr, wrb = wchunk(wbr2d, j * 128, KC=8)
                    psb_, bpb_ = ps_next(); proj_fm(wr, 0, 8, oT, boT, psb_, bpb_, wrb)
                    if br == 0:
                        V(lambda e, psb_=psb_: e.tensor_tensor(out=xt_t[:], in0=psb_[:], in1=tmpn, op=ALU.mult), r=[bpb_, b_arena], w=[b_xt])
                    else:
                        V(lambda e, psb_=psb_: e.tensor_tensor(out=tmpn, in0=psb_[:], in1=tmpn, op=ALU.mult), r=[bpb_, b_arena], w=[b_arena])
                        V(lambda e, j=j: e.tensor_tensor(out=merged[:, j, :], in0=tmpn, in1=xt_t[:], op=ALU.add), r=[b_arena, b_xt], w=[b_mg[j]])
            dump("mg%d" % l, merged[:, 0, :], [b_mg[0]])
            alias(b_big, allg + b_vb)
            wout = wout_d[l]
            for jp in range(8):
                wt, wb = wload(kview(wout, jp * 256, jp * 256 + 256, 16), [128, 16, 256])
                for fi in range(2):
                    j = jp * 2 + fi
                    psw, bpw = ps_next(); proj_fm(wt, fi * 128, 16, merged, b_mg, psw, bpw, wb)
                    A(lambda e, psw=psw, j=j: e.activation(out=big[:, j, :], in_=psw[:], func=AF.Identity), r=[bpw], w=[b_big[j]])
            S = sumsq_begin()
            for c in range(16):
                sumsq_add(S, big[:, c, :], [b_big[c]], 16)
            sumsq_finish(S, D)
            residual_apply(2)
            dump("x1_%d" % l, big[:, 0, :], [b_big[0]])

            S = sumsq_begin()
            for c in range(16):
                sumsq_add(S, big[:, c, :], [b_big[c]], 16)
            sumsq_finish(S, D)
            norm_apply(3, 4)
            for c in range(16):
                ld(xs_d[:, c, :], big[:, c, :], [], r=[b_big[c]])
            alias([b_U, b_dg[0], b_dg[1], b_sa, b_gT[0], b_gT[1]], b_mg)
            G(lambda e: e.memset(Uvar, 0.0), w=[b_U])
            wup = wup_d[l]; wdn = wdn_d[l]
            b_fs = [Buf("fs0"), Buf("fs1")]
            alias(b_fs, b_ob)
            fslots = [(wring[i][:, :], b_wr[i]) for i in range(NW)] + [(ar2[:, 16384 + i * 4096:16384 + (i + 1) * 4096], b_fs[i]) for i in range(2)]
            fctr = [0]

            def fload(src_ap, shape):
                i = fctr[0] % len(fslots)
                fctr[0] += 1
                flat, buf = fslots[i]
                n = 1
                for s_ in shape[1:]:
                    n *= s_
                view = flat[:, 0:n]
                if len(shape) == 3:
                    view = view.rearrange("p (a b) -> p a b", b=shape[2])
                P.dma("pool", lambda e: e.dma_start(out=view, in_=src_ap), writes=[buf])
                return view, buf

            units = [(g_, fi_, ty_) for g_ in range(NFC // 2) for fi_ in range(2) for ty_ in range(2)]
            wt_of = {}
            psu_of = {}
            dgc = [0]

            def ensure_w(u):
                g_, fi_, ty_ = u
                if (g_, ty_) not in wt_of:
                    c0 = ty_ * DFF + g_ * 256
                    wt_of[(g_, ty_)] = fload(kview(wup, c0, c0 + 256, 16), [128, 16, 256])

            def S1(u):
                g_, fi_, ty_ = u
                wt, wb = wt_of[(g_, ty_)]
                psu, bpu = ps_next()
                proj_fm(wt, fi_ * 128, 16, hT, b_hT, psu, bpu, wb)
                psu_of[u] = (psu, bpu)
                pinned.add(pst.index(psu))

            def S234(u):
                g_, fi_, ty_ = u
                j = g_ * 2 + fi_
                ci = ty_ * NFC + j
                gs = g_ % 2
                psu, bpu = psu_of.pop(u)
                A(lambda e: e.activation(out=Uvar[:, 0, PAD:PAD + T], in_=psu[:], func=AF.Identity), r=[bpu], w=[b_U])
                V(lambda e: e.tensor_tensor(out=Uvar[:, 1, PAD:PAD + T], in0=psu[:], in1=tmask[:, 2, :], op=ALU.mult), r=[bpu, b_const], w=[b_U])
                V(lambda e: e.tensor_tensor(out=Uvar[:, 2, PAD:PAD + T], in0=psu[:], in1=tmask[:, 3, :], op=ALU.mult), r=[bpu, b_const], w=[b_U])
                pinned.discard(pst.index(psu))
                dg = diag[dgc[0] % 2]; bdg = b_dg[dgc[0] % 2]; dgc[0] += 1
                for k in range(9):
                    V(lambda e, k=k: e.tensor_scalar(out=dg[:, k, :], in0=ident_b[:], scalar1=fcw[:, ci, k:k + 1], scalar2=None, op0=ALU.mult),
                      r=[b_const, b_lw], w=[bdg])
                psc, bpc = ps_next()
                for k in range(9):
                    ky, kx = k // 3, k % 3
                    dl = (ky - 1) * 64 + (kx - 1)
                    var = (1, 0, 2)[kx]
                    for h in range(2):
                        o0 = PAD + h * 512 + dl
                        M(lambda e, k=k, h=h, var=var, o0=o0: e.matmul(psc[:, h * 512:(h + 1) * 512], lhsT=dg[:, k, :], rhs=Uvar[:, var, o0:o0 + 512],
                                                                      start=(k == 0), stop=(k == 8)), r=[bdg, b_U], w=[bpc])
                if ty_ == 0:
                    A(lambda e: e.activation(out=sa_t, in_=psc[:], func=AF.Silu, bias=fcb[:, ci:ci + 1], scale=1.0), r=[bpc, b_lw], w=[b_sa])
                else:
                    V(lambda e: e.scalar_tensor_tensor(out=gT[gs][:, fi_, :], in0=psc[:], scalar=fcb[:, ci:ci + 1], in1=sa_t,
                                                       op0=ALU.add, op1=ALU.mult), r=[bpc, b_lw, b_sa], w=[b_gT[gs]])

            def downproj(g_):
                gs = g_ % 2
                wds = []
                for fi_ in range(2):
                    j = g_ * 2 + fi_
                    wds.append(fload(wdn[j * 128:(j + 1) * 128, :], [128, 2048]))
                for n in range(16):
                    psd, bpd = ps_next()
                    for fi_ in range(2):
                        for h in range(2):
                            M(lambda e, n=n, fi_=fi_, h=h, psd=psd, wd=wds[fi_][0]: e.matmul(psd[:, h * 512:(h + 1) * 512], lhsT=wd[:, n * 128:(n + 1) * 128],
                                                                                           rhs=gT[gs][:, fi_, h * 512:(h + 1) * 512], start=(fi_ == 0), stop=(fi_ == 1)),
                              r=[wds[fi_][1], b_gT[gs]], w=[bpd])
                    if g_ == 0:
                        V(lambda e, n=n, psd=psd: e.tensor_copy(out=big[:, n, :], in_=psd[:]), r=[bpd], w=[b_big[n]])
                    else:
                        V(lambda e, n=n, psd=psd: e.tensor_tensor(out=big[:, n, :], in0=psd[:], in1=big[:, n, :], op=ALU.add), r=[bpd, b_big[n]], w=[b_big[n]])

            b_fs_prev[:] = b_fs
            ensure_w(units[0]); S1(units[0])
            for ui, u in enumerate(units):
                if ui + 1 < len(units):
                    ensure_w(units[ui + 1]); S1(units[ui + 1])
                S234(u)
                if u[1] == 1 and u[2] == 1:
                    downproj(u[0])
            S = sumsq_begin()
            for c in range(16):
                sumsq_add(S, big[:, c, :], [b_big[c]], 16)
            sumsq_finish(S, D)
            residual_apply(5)
            dump("xo%d" % l, big[:, 0, :], [b_big[0]])

        yv = yT_d.rearrange("(c p) t -> p c t", p=128)
        for c in range(16):
            ld(yv[:, c, :], big[:, c, :], [], r=[b_big[c]])
        P.wait_all("sp", b_big + [b_const])
        P.emit()
    return nc


def _tables(is_sample):
    L = 1024 if is_sample else 256
    nseg = T // L
    t = np.arange(T)
    tl = t % L
    a = (tl[:, None] + 0.5) * (tl[None, :] + 0.5) * (math.pi / L)
    same = (t[:, None] // L) == (t[None, :] // L)
    MC = np.where(same, np.cos(a), 0.0).astype(np.float32)
    MS = np.where(same, np.sin(a), 0.0).astype(np.float32)
    b = tl[:, None] * (tl[None, :] + 0.5) * (math.pi / L)
    FC = np.where(same, np.cos(b), 0.0).astype(np.float32)
    FS = np.where(same, np.sin(b), 0.0).astype(np.float32)
    t01 = tl / max(L - 1, 1)
    m0 = (tl == 0).astype(np.float64)
    rowt = np.stack([t01, m0, 1.0 - m0, np.full(T, 1.0 / L)], axis=-1)
    rowt = rowt.reshape(8, 128, 4).transpose(1, 0, 2).astype(np.float32)
    cmask_f = (t % 32 != 0).astype(np.float32)
    if is_sample:
        maskL = (t % 64 != 63); maskR = (t % 64 != 0)
    else:
        maskL = (t % 256 != 255); maskR = (t % 256 != 0)
    tmask = np.stack([cmask_f, cmask_f, maskL.astype(np.float32), maskR.astype(np.float32)], 0).astype(np.float32)
    bands = np.linspace(1e-4, 16 - 1, 16).astype(np.float32)
    ang = (2.0 * math.pi / L) * tl[:, None].astype(np.float32) * bands[None, :]
    feats = np.concatenate([t01[:, None].astype(np.float32), np.cos(ang), -np.sin(ang)], axis=-1).astype(np.float32)
    cf = 1.0 if is_sample else 0.0
    cfg = np.zeros((128, 8), np.float32)
    cfg[:, 0] = cf; cfg[:, 1] = 1.0 - cf; cfg[:, 2] = 1.0 / nseg
    p = np.arange(128)
    same_c = (p[:, None] // 32) == (p[None, :] // 32)
    trif = (same_c & (p[:, None] <= p[None, :])).astype(np.float32)
    trib = (same_c & (p[:, None] >= p[None, :])).astype(np.float32)
    return dict(cfg=cfg, rowt=rowt, tmask=tmask, featsT=np.ascontiguousarray(feats.T), MC=MC, MS=MS, FC=FC, FS=FS,
                trif=trif, trib=trib)


def prep_inputs(x_prompt, x_sample, state_hgrn, c, c_ctx, w_mod, b_mod, g_pre_mix, g_post_mix,
                g_pre_ffn, g_post_ffn, w_in, hgrn_lower_bounds, hgrn_norm, hy_conv_w, hy_conv_b,
                hy_w1, hy_b1, hy_w2, hy_b2, hy_w3, hy_decay, hy_bias, w_branch_a, w_branch_b,
                w_out, ffn_w_up, ffn_conv_w, ffn_conv_b, ffn_w_down):
    f = lambda a: np.ascontiguousarray(np.asarray(a, dtype=np.float32))

    def pc(v, n):
        v = f(v)
        return np.ascontiguousarray(np.swapaxes(v.reshape(v.shape[:-1] + (n, 128)), -1, -2))

    shared = dict(
        w_mod=f(w_mod), b_modT=pc(b_mod, 96),
        gvec=np.ascontiguousarray(np.stack([pc(g_pre_mix, 16), pc(g_post_mix, 16), pc(g_pre_ffn, 16), pc(g_post_ffn, 16)], axis=2)),
        w_in=f(w_in),
        hlb=np.ascontiguousarray(pc(f(hgrn_lower_bounds).reshape(DEPTH, 2048), 16)),
        hnorm=f(hgrn_norm).reshape(DEPTH, 128, 1),
        hcw=np.ascontiguousarray(pc(f(hy_conv_w), 24).transpose(0, 2, 3, 1)),
        hcb=pc(hy_conv_b, 24),
        hw1=f(hy_w1), hb1=f(hy_b1).reshape(DEPTH, 64, 1), hw2=f(hy_w2), hb2=f(hy_b2).reshape(DEPTH, 64, 1), hw3=f(hy_w3),
        hdec=f(hy_decay), hbias=np.ascontiguousarray(pc(f(hy_bias), 8).transpose(0, 2, 1, 3)),
        wba=f(w_branch_a), wbb=f(w_branch_b), wout=f(w_out), wup=f(ffn_w_up),
        fcw=np.ascontiguousarray(pc(f(ffn_conv_w).reshape(DEPTH, 9, 2 * DFF), 88).transpose(0, 2, 3, 1)),
        fcb=pc(ffn_conv_b, 88), wdn=f(ffn_w_down),
    )
    tabs = {True: _tables(True), False: _tables(False)}
    xs = f(x_sample); xp = f(x_prompt); sh = f(state_hgrn)
    maps = []
    for core in range(8):
        is_s = core < 4
        m = dict(shared)
        m.update(tabs[is_s])
        if is_s:
            m["xT"] = np.ascontiguousarray(xs[core].T)
            m["s0"] = np.ascontiguousarray(sh[core])
            m["cond"] = pc(f(c)[core], 16)
        else:
            j = core - 4
            m["xT"] = np.ascontiguousarray(xp[4 * j:4 * j + 4].reshape(T, D).T)
            m["s0"] = np.zeros((DEPTH, 2, 8, 128, 128), np.float32)
            m["cond"] = pc(f(c_ctx), 16)
        maps.append(m)
    return maps


_NC = [None]


def kernel(**inputs):
    maps = prep_inputs(**inputs)
    if _NC[0] is None:
        _NC[0] = build()
    res = run_bass_kernel_spmd(_NC[0], maps, core_ids=list(range(8)))
    R = res.results
    y_sample = np.stack([np.ascontiguousarray(R[i]["yT"].T) for i in range(4)], 0).astype(np.float32)
    y_prompt = np.concatenate([np.ascontiguousarray(R[4 + j]["yT"].T).reshape(4, 256, D) for j in range(4)], 0).astype(np.float32)
    ns = np.concatenate([np.ascontiguousarray(R[4 + j]["st"].transpose(1, 0, 2, 3, 4, 5)) for j in range(4)], 0).astype(np.float32)
    return (y_prompt, y_sample, ns)
```

```python
import math
from contextlib import ExitStack
import numpy as np
import concourse.bass as bass
import concourse.mybir as mybir
from concourse.bass_utils import run_bass_kernel_spmd

F32 = mybir.dt.float32
BF16 = mybir.dt.bfloat16
AF = mybir.ActivationFunctionType
ALU = mybir.AluOpType

D = 2048
T = 1024
DEPTH = 2
D_A = 1024
NPROJ = 12288
DFF = 5632
NFC = 44
EPS = 1e-6
PAD = 66


class Buf:
    __slots__ = ("name", "w", "r")

    def __init__(self, name=""):
        self.name = name
        self.w = None
        self.r = {}


class Prog:
    ENG = ("pe", "act", "dve", "pool", "sp")

    def __init__(self, nc):
        self.nc = nc
        self.lists = {k: [] for k in self.ENG}
        self.cnt = {k: 0 for k in self.ENG}
        self.seen = {k: {} for k in self.ENG}
        self.semh = {}
        self.dma_pool = {}
        self.dma_next = {}
        self.dma_val = {}
        self.n_dma_sems = {"sp": 24, "pool": 16, "act": 4}

    def alloc_sems(self, stack):
        for k in self.ENG:
            self.semh[k] = stack.enter_context(self.nc.semaphore("s_" + k))
        for q, n in self.n_dma_sems.items():
            keys = []
            for i in range(n):
                key = "d_%s_%d" % (q, i)
                self.semh[key] = stack.enter_context(self.nc.semaphore(key))
                self.dma_val[key] = 0
                keys.append(key)
            self.dma_pool[q] = keys
            self.dma_next[q] = 0

    def _deps(self, e, reads, writes, acc):
        deps = {}

        def add(ev):
            if ev is None:
                return
            k, v = ev
            if deps.get(k, 0) < v:
                deps[k] = v

        for b in reads:
            add(b.w)
        for b in writes:
            if not (acc and b.w is not None and b.w[0] == e):
                add(b.w)
            for k, v in b.r.items():
                if not (acc and k == e):
                    add((k, v))
        waits = []
        seen = self.seen[e]
        for k, v in deps.items():
            if seen.get(k, 0) < v:
                seen[k] = v
                waits.append((k, v))
        return waits

    def op(self, e, fn, reads=(), writes=(), acc=False):
        waits = self._deps(e, reads, writes, acc)
        self.cnt[e] += 1
        v = self.cnt[e]
        self.lists[e].append((waits, fn, (e, 1)))
        for b in reads:
            if b.r.get(e, 0) < v:
                b.r[e] = v
        for b in writes:
            b.w = (e, v)
            b.r = {}

    def dma(self, q, fn, reads=(), writes=()):
        pool = self.dma_pool[q]
        key = pool[self.dma_next[q] % len(pool)]
        self.dma_next[q] += 1
        waits = self._deps(q, reads, writes, False)
        pv = self.dma_val[key]
        if pv > 0 and self.seen[q].get(key, 0) < pv:
            self.seen[q][key] = pv
            waits.append((key, pv))
        v = pv + 16
        self.dma_val[key] = v
        self.lists[q].append((waits, fn, (key, 16)))
        for b in reads:
            if b.r.get(key, 0) < v:
                b.r[key] = v
        for b in writes:
            b.w = (key, v)
            b.r = {}

    def wait_all(self, e, bufs):
        deps = {}
        for b in bufs:
            if b.w is not None:
                deps[b.w[0]] = max(deps.get(b.w[0], 0), b.w[1])
            for k, v in b.r.items():
                deps[k] = max(deps.get(k, 0), v)
        waits = [(k, v) for k, v in deps.items() if self.seen[e].get(k, 0) < v]
        for k, v in waits:
            self.seen[e][k] = v
        self.lists[e].append((waits, None, None))

    def emit(self):
        nc = self.nc
        with nc.Block() as block:
            def mk(e):
                def body(engine):
                    for waits, fn, inc in self.lists[e]:
                        for k, v in waits:
                            engine.wait_ge(self.semh[k], v)
                        if fn is not None:
                            fn(engine).then_inc(self.semh[inc[0]], inc[1])
                return body
            block.tensor(mk("pe"))
            block.scalar(mk("act"))
            block.vector(mk("dve"))
            block.gpsimd(mk("pool"))
            block.sync(mk("sp"))


def build(n_layers=DEPTH, dbg=None):
    nc = bass.Bass("TRN2", target_bir_lowering=False)
    dbg = dbg or {}

    def din(name, shape):
        return nc.dram_tensor(name, list(shape), F32, kind="ExternalInput").ap()

    def dout(name, shape):
        return nc.dram_tensor(name, list(shape), F32, kind="ExternalOutput").ap()

    xT_d = din("xT", [D, T])
    yT_d = dout("yT", [D, T])
    st_d = dout("st", [DEPTH, 4, 2, 8, 128, 128])
    s0_d = din("s0", [DEPTH, 2, 8, 128, 128])
    cond_d = din("cond", [128, 16])
    cfg_d = din("cfg", [128, 8])
    rowt_d = din("rowt", [128, 8, 4])
    tmask_d = din("tmask", [4, T])
    feats_d = din("featsT", [33, T])
    MC_d = din("MC", [T, T]); MS_d = din("MS", [T, T]); FC_d = din("FC", [T, T]); FS_d = din("FS", [T, T])
    trif_d = din("trif", [128, 128]); trib_d = din("trib", [128, 128])
    w_mod_d = din("w_mod", [DEPTH, D, 6 * D]); b_mod_d = din("b_modT", [DEPTH, 128, 96])
    gvec_d = din("gvec", [DEPTH, 128, 4, 16])
    w_in_d = din("w_in", [DEPTH, D, NPROJ])
    hlb_d = din("hlb", [DEPTH, 128, 16])
    hnorm_d = din("hnorm", [DEPTH, 128, 1])
    hcw_d = din("hcw", [DEPTH, 128, 24, 3]); hcb_d = din("hcb", [DEPTH, 128, 24])
    hw1_d = din("hw1", [DEPTH, 33, 64]); hb1_d = din("hb1", [DEPTH, 64, 1])
    hw2_d = din("hw2", [DEPTH, 64, 64]); hb2_d = din("hb2", [DEPTH, 64, 1])
    hw3_d = din("hw3", [DEPTH, 64, 4096])
    hdec_d = din("hdec", [DEPTH, 2, 1024]); hbias_d = din("hbias", [DEPTH, 128, 2, 8])
    wba_d = din("wba", [DEPTH, 1024, D]); wbb_d = din("wbb", [DEPTH, 1024, D])
    wout_d = din("wout", [DEPTH, D, D])
    wup_d = din("wup", [DEPTH, D, 2 * DFF])
    fcw_d = din("fcw", [DEPTH, 128, 88, 9]); fcb_d = din("fcb", [DEPTH, 128, 88])
    wdn_d = din("wdn", [DEPTH, DFF, D])
    xs_d = nc.dram_tensor("xspill", [128, 16, T], F32).ap()
    dbg_d = {k: dout("dbg_" + k, shp) for k, shp in dbg.items()}

    with ExitStack() as st:
        P = Prog(nc)
        P.alloc_sems(st)

        def sb(name, shape, dt=F32):
            return st.enter_context(nc.sbuf_tensor("sb_" + name, list(shape), dt))

        big = sb("big", [128, 16, T], F32)
        b_big = [Buf("big%d" % c) for c in range(16)]
        hT = sb("hT", [128, 16, T], BF16)
        b_hT = [Buf("hT%d" % c) for c in range(16)]
        NW = 3
        wring = [sb("wr%d" % i, [128, 4096], BF16) for i in range(NW)]
        b_wr = [Buf("wr%d" % i) for i in range(NW)]
        wctr = [0]
        tmpn_t = sb("tmpn", [128, T], F32)
        xt_t = sb("xt", [128, T], F32)
        b_arena = Buf("tmpn"); b_xt = Buf("xt")
        ar2 = sb("ar2", [128, 24576], BF16)
        UW = T + 2 * PAD
        Uvar = ar2[:, 0:3 * UW].rearrange("p (a b) -> p a b", b=UW); b_U = Buf("Uvar")
        o_ = 3 * UW + (-(3 * UW) % 64)
        diag = [ar2[:, o_ + i * 1152:o_ + (i + 1) * 1152].rearrange("p (a b) -> p a b", b=128) for i in range(2)]; b_dg = [Buf("dg0"), Buf("dg1")]
        o_ += 2304
        sa_t = ar2[:, o_:o_ + 2048].bitcast(F32); b_sa = Buf("sa")
        o_ += 2048
        gT = [ar2[:, o_ + i * 2048:o_ + (i + 1) * 2048].rearrange("p (a b) -> p a b", b=T) for i in range(2)]; b_gT = [Buf("gT0"), Buf("gT1")]
        MC = ar2[:, 0:8192].rearrange("p (a b) -> p a b", b=T); MS = ar2[:, 8192:16384].rearrange("p (a b) -> p a b", b=T)
        b_MC = Buf("MC")
        merged = ar2[:, 0:16384].rearrange("p (a b) -> p a b", b=T); b_mg = [Buf("mg%d" % i) for i in range(16)]
        obT = ar2[:, 16384:24576].rearrange("p (a b) -> p a b", b=T); b_ob = [Buf("ob%d" % i) for i in range(8)]
        bigf = big[:].rearrange("p a b -> p (a b)")
        b_R1 = Buf("R1")

        def cf32(off, n):
            return bigf[:, off:off + n]

        def cbf(off, n):
            return bigf[:, off:off + n // 2].bitcast(BF16)

        ident_b = sb("ident_b", [128, 128], BF16); ident_f = sb("ident_f", [128, 128], F32)
        ones_b = sb("ones_b", [128, 128], BF16)
        b_const = Buf("const")
        cfg = sb("cfg", [128, 8]); rowt = sb("rowt", [128, 8, 4])
        tmask = sb("tmask", [128, 4, T], BF16)
        trif = sb("trif", [128, 128]); trib = sb("trib", [128, 128])
        scond = sb("scond", [128, 16], BF16)
        condt = sb("condt", [128, 16])
        modT = sb("modT", [128, 96]); b_mod = Buf("modT")
        bmodT = sb("bmodT", [128, 96])
        gvec = sb("gvec", [128, 4, 16])
        AB = sb("AB", [128, 6, 16]); b_AB = Buf("AB")
        lbt = sb("lbt", [128, 3, 16]); b_lbt = Buf("lbt")
        hlb = sb("hlb", [128, 2, 16])
        hnorm = sb("hnorm", [128, 1])
        hcw = sb("hcw", [128, 24, 3]); hcb = sb("hcb", [128, 24]); hcwc = sb("hcwc", [128, 24, 2])
        hbias = sb("hbias", [128, 2, 8])
        fcw = sb("fcw", [128, 88, 9]); fcb = sb("fcb", [128, 88])
        b_lw = Buf("layerw")
        rstd = sb("rstd", [128, T]); b_rstd = Buf("rstd")
        sq = [sb("sq%d" % i, [128, T], BF16) for i in range(2)]; b_sq = [Buf("sq0"), Buf("sq1")]
        sqc = [0]

        pst = [st.enter_context(nc.psum_tensor("ps%d" % i, [128, 1024], F32)) for i in range(4)]
        b_ps = [Buf("ps%d" % i) for i in range(4)]

        E = lambda name: name

        def V(fn, r=(), w=(), acc=False):
            P.op("dve", fn, r, w, acc)

        def A(fn, r=(), w=(), acc=False):
            P.op("act", fn, r, w, acc)

        def G(fn, r=(), w=(), acc=False):
            P.op("pool", fn, r, w, acc)

        def M(fn, r=(), w=(), acc=True):
            P.op("pe", fn, r, w, acc)

        def ld(dst, src, w, q="sp", r=()):
            P.dma(q, lambda e: e.dma_start(out=dst, in_=src), reads=r, writes=w)

        def wload(src_ap, shape):
            i = wctr[0] % NW
            wctr[0] += 1
            n = 1
            for s in shape[1:]:
                n *= s
            assert n <= 4096
            flat = wring[i][:, 0:n]
            if len(shape) == 3:
                view = flat.rearrange("p (a b) -> p a b", b=shape[2])
            else:
                view = flat
            P.dma("pool", lambda e: e.dma_start(out=view, in_=src_ap), writes=[b_wr[i]])
            return view, b_wr[i]

        def kview(w2d, c0, c1, KC):
            return w2d.rearrange("(c p) n -> p c n", p=128)[:, 0:KC, c0:c1]

        G(lambda e: e.memset(ones_b[:], 1.0), w=[b_const])
        G(lambda e: e.memset(ident_f[:], 1.0), w=[b_const])
        G(lambda e: e.affine_select(out=ident_f[:], in_=ident_f[:], pattern=[[-1, 128]], compare_op=ALU.is_equal,
                                    fill=0.0, base=0, channel_multiplier=1), r=[b_const], w=[b_const])
        G(lambda e: e.tensor_copy(out=ident_b[:], in_=ident_f[:]), r=[b_const], w=[b_const])
        ld(cfg[:], cfg_d, [b_const]); ld(rowt[:], rowt_d, [b_const])
        ld(trif[:], trif_d, [b_const]); ld(trib[:], trib_d, [b_const])
        ld(condt[:], cond_d, [b_const])
        P.dma("pool", lambda e: e.dma_start(out=tmask[:], in_=tmask_d.partition_broadcast(128)), writes=[b_const])
        A(lambda e: e.activation(out=scond[:], in_=condt[:], func=AF.Silu), r=[b_const], w=[b_const])

        def dump(name, ap, bufs):
            if name in dbg_d:
                ld(dbg_d[name], ap, [], q=("sp" if ap.dtype == F32 else "pool"), r=bufs)

        psn = [4]

        pinned = set()

        def ps_next(ctr=[0]):
            while True:
                i = ctr[0] % psn[0]
                ctr[0] += 1
                if i not in pinned:
                    return pst[i], b_ps[i]

        def sumsq_begin():
            ps, bp = ps_next()
            return {"ps": ps, "bp": bp, "n": 0}

        def sumsq_add(S, src_ap, src_bufs, total):
            i = sqc[0] % 2
            sqc[0] += 1
            A(lambda e: e.activation(out=sq[i][:], in_=src_ap, func=AF.Square), r=src_bufs, w=[b_sq[i]])
            first = S["n"] == 0
            last = S["n"] == total - 1
            for h in range(2):
                M(lambda e, h=h: e.matmul(S["ps"][:, h * 512:(h + 1) * 512], lhsT=ones_b[:], rhs=sq[i][:, h * 512:(h + 1) * 512],
                                          start=first, stop=last), r=[b_sq[i], b_const], w=[S["bp"]])
            S["n"] += 1

        def sumsq_finish(S, n_feat):
            A(lambda e: e.activation(out=rstd[:], in_=S["ps"][:], func=AF.Sqrt, scale=1.0 / n_feat, bias=epsb[:]),
              r=[S["bp"], b_const], w=[b_rstd])
            V(lambda e: e.reciprocal(out=rstd[:], in_=rstd[:]), r=[b_rstd], w=[b_rstd])

        rmask = sb("rmask", [128, 4])
        G(lambda e: e.memset(rmask[:], 0.0), w=[b_const])
        for j_ in range(4):
            G(lambda e, j_=j_: e.memset(rmask[32 * j_:32 * j_ + 32, j_:j_ + 1], 1.0), r=[b_const], w=[b_const])
        epsb = sb("epsb", [128, 1])
        G(lambda e: e.memset(epsb[:], EPS), w=[b_const])
        negpi = sb("negpi", [128, 1])
        G(lambda e: e.memset(negpi[:], -math.pi), w=[b_const])

        def proj_fm(wt, col, KC, rhs, rhs_bufs, ps, bp, wb):
            for c in range(KC):
                for h in range(2):
                    M(lambda e, c=c, h=h: e.matmul(ps[:, h * 512:(h + 1) * 512], lhsT=wt[:, c, col:col + 128],
                                                   rhs=rhs[:, c, h * 512:(h + 1) * 512], start=(c == 0), stop=(c == KC - 1)),
                      r=[wb, rhs_bufs[c]], w=[bp])

        tmpn = tmpn_t[:]

        def norm_apply(Ai, Bi):
            for c in range(16):
                V(lambda e, c=c: e.scalar_tensor_tensor(out=tmpn, in0=big[:, c, :], scalar=AB[:, Ai, c:c + 1], in1=rstd[:],
                                                        op0=ALU.mult, op1=ALU.mult), r=[b_big[c], b_AB, b_rstd], w=[b_arena])
                A(lambda e, c=c: e.activation(out=hT[:, c, :], in_=tmpn, func=AF.Identity, bias=AB[:, Bi, c:c + 1], scale=1.0),
                  r=[b_arena, b_AB], w=[b_hT[c]])

        def residual_apply(Gi):
            xt = xt_t[:]
            for c in range(16):
                ld(xt, xs_d[:, c, :], [b_xt], r=[b_big[c]])
                V(lambda e, c=c: e.scalar_tensor_tensor(out=tmpn, in0=big[:, c, :], scalar=AB[:, Gi, c:c + 1], in1=rstd[:],
                                                        op0=ALU.mult, op1=ALU.mult), r=[b_big[c], b_AB, b_rstd], w=[b_arena])
                V(lambda e, c=c: e.tensor_tensor(out=big[:, c, :], in0=tmpn, in1=xt, op=ALU.add), r=[b_arena, b_xt], w=[b_big[c]])

        nt01 = sb("nt01", [128, 8]); hb12 = sb("hb12", [64, 2])
        xv = xT_d.rearrange("(c p) t -> p c t", p=128)
        for c in range(16):
            ld(big[:, c, :], xv[:, c, :], [b_big[c]])

        b_fs_prev = []
        for l in range(n_layers):
            ld(bmodT[:], b_mod_d[l], [b_lw], r=[b_lw]); ld(gvec[:], gvec_d[l], [b_lw])
            ld(hnorm[:], hnorm_d[l], [b_lw]); ld(hcw[:], hcw_d[l], [b_lw]); ld(hcb[:], hcb_d[l], [b_lw])
            ld(hbias[:], hbias_d[l], [b_lw]); ld(fcw[:], fcw_d[l], [b_lw]); ld(fcb[:], fcb_d[l], [b_lw])
            if l == 0:
                ld(hlb[:, 0, :], hlb_d[0], [b_lw]); ld(hlb[:, 1, :], hlb_d[1], [b_lw])
                G(lambda e: e.memset(lbt[:, 0, :], 0.0), w=[b_lbt])
            else:
                V(lambda e: e.tensor_tensor(out=lbt[:, 0, :], in0=hlb[:, 1, :], in1=hlb[:, 0, :], op=ALU.subtract), r=[b_lw, b_lbt], w=[b_lbt])
                A(lambda e: e.activation(out=lbt[:, 0, :], in_=lbt[:, 0, :], func=AF.Sigmoid), r=[b_lbt], w=[b_lbt])
            V(lambda e: e.tensor_scalar(out=lbt[:, 1, :], in0=lbt[:, 0, :], scalar1=-1.0, scalar2=1.0, op0=ALU.mult, op1=ALU.add), r=[b_lbt], w=[b_lbt])
            V(lambda e: e.tensor_scalar(out=lbt[:, 2, :], in0=lbt[:, 0, :], scalar1=1.0, scalar2=-1.0, op0=ALU.mult, op1=ALU.add), r=[b_lbt], w=[b_lbt])
            V(lambda e: e.tensor_scalar(out=hcwc[:, :, 0:1], in0=hcw[:, :, 0:1], scalar1=cfg[:, 0:1], scalar2=None, op0=ALU.mult), r=[b_lw, b_const], w=[b_lw])
            V(lambda e: e.tensor_scalar(out=hcwc[:, :, 1:2], in0=hcw[:, :, 2:3], scalar1=cfg[:, 0:1], scalar2=None, op0=ALU.mult), r=[b_lw, b_const], w=[b_lw])
            for k in (0, 1, 2, 6, 7, 8):
                V(lambda e, k=k: e.tensor_scalar(out=fcw[:, :, k:k + 1], in0=fcw[:, :, k:k + 1], scalar1=cfg[:, 0:1], scalar2=None, op0=ALU.mult), r=[b_lw, b_const], w=[b_lw])

            psm, bpm = ps_next()
            first = True
            for ng in range(6):
                for c in range(16):
                    wt, wb = wload(w_mod_d[l][c * 128:(c + 1) * 128, ng * 2048:(ng + 1) * 2048], [128, 2048])
                    for jj in range(16):
                        j = ng * 16 + jj
                        M(lambda e, jj=jj, j=j, c=c, wt=wt, f=first: e.matmul(psm[:, j:j + 1], lhsT=wt[:, jj * 128:(jj + 1) * 128], rhs=scond[:, c:c + 1],
                                                                              start=f, stop=(ng == 5 and c == 15 and jj == 15), skip_group_check=True),
                          r=[wb, b_const], w=[bpm])
                        first = False
            V(lambda e: e.tensor_tensor(out=modT[:], in0=psm[:, 0:96], in1=bmodT[:], op=ALU.add), r=[bpm, b_lw], w=[b_mod])
            for s in range(2):
                o = 48 * s
                V(lambda e, s=s, o=o: e.scalar_tensor_tensor(out=AB[:, 3 * s + 0, :], in0=modT[:, o + 16:o + 32], scalar=1.0, in1=gvec[:, 2 * s, :],
                                                             op0=ALU.add, op1=ALU.mult), r=[b_mod, b_lw], w=[b_AB])
                V(lambda e, s=s, o=o: e.tensor_copy(out=AB[:, 3 * s + 1, :], in_=modT[:, o:o + 16]), r=[b_mod], w=[b_AB])
                V(lambda e, s=s, o=o: e.tensor_tensor(out=AB[:, 3 * s + 2, :], in0=modT[:, o + 32:o + 48], in1=gvec[:, 2 * s + 1, :], op=ALU.mult), r=[b_mod, b_lw], w=[b_AB])
            dump("modT%d" % l, modT[:], [b_mod])

            S = sumsq_begin()
            for c in range(16):
                sumsq_add(S, big[:, c, :], [b_big[c]], 16)
            sumsq_finish(S, D)
            norm_apply(0, 1)
            for c in range(16):
                ld(xs_d[:, c, :], big[:, c, :], [], r=[b_big[c]])
            b_xs = b_big
            dump("hT%d" % l, hT[:, 0, :], [b_hT[0]])
            w_in = w_in_d[l]


            def alias(new, old):
                for nb in new:
                    for ob in old:
                        if ob.w is not None:
                            k, v = ob.w
                            if nb.r.get(k, 0) < v:
                                nb.r[k] = v
                        for k, v in ob.r.items():
                            if nb.r.get(k, 0) < v:
                                nb.r[k] = v

            def wchunk(w2d, col, KC=16):
                return wload(kview(w2d, col, col + 128, KC), [128, KC, 128])

            def psbf(ps):
                return ps[:, 0:512].bitcast(BF16)

            HB = {k: Buf("hy_" + k) for k in ("vT", "x1T", "x2T", "Kp", "t12", "t34", "hwj", "win", "hfp", "hbp", "hp", "hm", "ab",
                                              "ztok", "zbf", "P", "Q", "rinv", "decb")}
            alias(list(HB.values()), b_big)
            alias([b_MC], [b_U, b_dg[0], b_dg[1], b_sa, b_gT[0], b_gT[1]] + b_mg)
            alias(b_ob, b_fs_prev)
            vT = cf32(0, 1024); x1T = cf32(1024, 1024); x2T = cf32(2048, 1024)
            Kp = cf32(3072, 4096).rearrange("p (f c n) -> p f c n", c=2, n=256)
            t12 = cf32(7168, 1024); t34 = cf32(8192, 1024)
            hwj = cf32(9216, 512); win = cf32(9728, 256); hfp = cf32(9984, 256); hbp = cf32(10240, 256)
            hp = cbf(10496, 2048).rearrange("p (a b) -> p a b", b=256)
            hm = cbf(11520, 2048).rearrange("p (a b) -> p a b", b=256)
            ab = cbf(12544, 2048).rearrange("p (a b) -> p a b", b=256)
            ztok = cbf(13568, 1024).rearrange("p (a b) -> p a b", b=128)
            zbf = cbf(14080, 1024)
            Pq = cbf(14592, 1024).rearrange("p (a b) -> p a b", b=128)
            Qq = cbf(15104, 1024).rearrange("p (a b) -> p a b", b=128)
            rinv = cf32(15616, 256); decb = cf32(15872, 256)
            xtb = xt_t[:].bitcast(BF16)
            tnb = tmpn_t[:].bitcast(BF16)
            h1T = xtb[0:64, 0:T]; h2T = xtb[0:64, T:2 * T]
            featsT = tnb[0:33, 0:T]; w1b = tnb[0:33, T:T + 64]; w2b = tnb[0:64, T + 64:T + 128]
            w3c = tnb[0:64, T + 128:T + 640]
            ld(MC, MC_d.rearrange("(a p) b -> p a b", p=128), [b_MC], q="pool")
            ld(MS, MS_d.rearrange("(a p) b -> p a b", p=128), [b_MC], q="pool")
            ld(featsT, feats_d, [b_arena], q="pool")
            ld(w1b, hw1_d[l], [b_arena], q="pool"); ld(w2b, hw2_d[l], [b_arena], q="pool")
            ld(hb12[:, 0:1], hb1_d[l], [b_lw], r=[b_lw]); ld(hb12[:, 1:2], hb2_d[l], [b_lw], r=[b_lw])
            V(lambda e: e.tensor_scalar(out=nt01[:], in0=rowt[:, :, 0], scalar1=-1.0, scalar2=None, op0=ALU.mult), r=[b_const], w=[b_lw])
            pre = t12[0:64, :]
            for li, (wl, src, dst, KK) in enumerate(((w1b, featsT, h1T, 33), (w2b, h1T, h2T, 64))):
                ps, bp = ps_next()
                for h in range(2):
                    M(lambda e, h=h, ps=ps, wl=wl, src=src: e.matmul(ps[0:64, h * 512:(h + 1) * 512], lhsT=wl, rhs=src[:, h * 512:(h + 1) * 512], start=True, stop=True),
                      r=[b_arena, b_xt], w=[bp])
                V(lambda e, ps=ps, li=li: e.tensor_scalar(out=pre, in0=ps[0:64, :], scalar1=hb12[:, li:li + 1], scalar2=None, op0=ALU.add), r=[bp, b_lw], w=[HB["t12"]])
                mA = t34[0:64, :]; mB = cf32(3072, 1024)[0:64, :]
                V(lambda e: e.tensor_scalar(out=mA, in0=pre, scalar1=-math.pi, scalar2=2 * math.pi, op0=ALU.is_lt, op1=ALU.mult), r=[HB["t12"]], w=[HB["t34"]])
                V(lambda e: e.tensor_scalar(out=mB, in0=pre, scalar1=math.pi, scalar2=-2 * math.pi, op0=ALU.is_gt, op1=ALU.mult), r=[HB["t12"]], w=[HB["Kp"]])
                V(lambda e: e.tensor_tensor(out=pre, in0=pre, in1=mA, op=ALU.add), r=[HB["t12"], HB["t34"]], w=[HB["t12"]])
                V(lambda e: e.tensor_tensor(out=pre, in0=pre, in1=mB, op=ALU.add), r=[HB["t12"], HB["Kp"]], w=[HB["t12"]])
                A(lambda e, dst=dst: e.activation(out=dst, in_=pre, func=AF.Sin), r=[HB["t12"]], w=[b_xt])
            w3v = hw3_d[l].rearrange("k (g c) -> k g c", c=1024)
            for cc in range(8):
                def proj_typ(typ, cc=cc):
                    dst, bdst = ((vT, HB["vT"]), (x1T, HB["x1T"]), (x2T, HB["x2T"]))[typ]
                    ci = typ * 8 + cc
                    wt, wb = wchunk(w_in, 5120 + typ * 1024 + cc * 128)
                    ps, bp = ps_next()
                    proj_fm(wt, 0, 16, hT, b_hT, ps, bp, wb)
                    A(lambda e, ps=ps, dst=dst, ci=ci: e.activation(out=dst, in_=ps[:], func=AF.Identity, scale=hcw[:, ci, 1:2], bias=hcb[:, ci:ci + 1]),
                      r=[bp, b_lw], w=[bdst])
                    d3 = dst.rearrange("p (s i) -> p s i", i=256)
                    p3 = ps[:].rearrange("p (s i) -> p s i", i=256)
                    V(lambda e, d3=d3, p3=p3, ci=ci: e.scalar_tensor_tensor(out=d3[:, :, 1:256], in0=p3[:, :, 0:255], scalar=hcw[:, ci, 0:1], in1=d3[:, :, 1:256],
                                                                           op0=ALU.mult, op1=ALU.add), r=[bp, b_lw, bdst], w=[bdst])
                    V(lambda e, d3=d3, p3=p3, ci=ci: e.scalar_tensor_tensor(out=d3[:, 1:4, 0:1], in0=p3[:, 0:3, 255:256], scalar=hcwc[:, ci, 0:1], in1=d3[:, 1:4, 0:1],
                                                                           op0=ALU.mult, op1=ALU.add), r=[bp, b_lw, bdst], w=[bdst])
                    V(lambda e, d3=d3, p3=p3, ci=ci: e.scalar_tensor_tensor(out=d3[:, :, 0:255], in0=p3[:, :, 1:256], scalar=hcw[:, ci, 2:3], in1=d3[:, :, 0:255],
                                                                           op0=ALU.mult, op1=ALU.add), r=[bp, b_lw, bdst], w=[bdst])
                    V(lambda e, d3=d3, p3=p3, ci=ci: e.scalar_tensor_tensor(out=d3[:, 0:3, 255:256], in0=p3[:, 1:4, 0:1], scalar=hcwc[:, ci, 1:2], in1=d3[:, 0:3, 255:256],
                                                                           op0=ALU.mult, op1=ALU.add), r=[bp, b_lw, bdst], w=[bdst])
                ld(w3c.rearrange("k (g c) -> k g c", c=128), w3v[:, :, cc * 128:(cc + 1) * 128], [b_arena], q="pool", r=[b_arena])
                ld(decb.rearrange("p (o c) -> p o c", c=128), hdec_d[l][:, cc * 128:(cc + 1) * 128].partition_broadcast(128), [HB["decb"]])
                A(lambda e: e.activation(out=decb, in_=decb, func=AF.Abs), r=[HB["decb"]], w=[HB["decb"]])
                hw4 = hwj.rearrange("p (o f c) -> p o f c", f=2, c=128)
                win4 = win.rearrange("p (o c) -> p o c", c=128).unsqueeze(2).to_broadcast([128, 2, 2, 128])
                hf3 = hw4[:, :, 0, :]; hb3 = hw4[:, :, 1, :]
                hfp3 = hfp.rearrange("p (o c) -> p o c", c=128); hbp3 = hbp.rearrange("p (o c) -> p o c", c=128)
                for jt in range(8):
                    ps, bp = ps_next()
                    M(lambda e, ps=ps, jt=jt: e.matmul(ps[:, 0:512], lhsT=h2T[:, jt * 128:(jt + 1) * 128], rhs=w3c, start=True, stop=True), r=[b_xt, b_arena], w=[bp])
                    A(lambda e, jt=jt: e.activation(out=win, in_=decb, func=AF.Exp, scale=nt01[:, jt:jt + 1]), r=[HB["decb"], b_lw], w=[HB["win"]])
                    V(lambda e, ps=ps: e.tensor_tensor(out=hw4, in0=ps[:, 0:512].rearrange("p (o f c) -> p o f c", f=2, c=128), in1=win4, op=ALU.mult),
                      r=[bp, HB["win"]], w=[HB["hwj"]])
                    V(lambda e, jt=jt: e.scalar_tensor_tensor(out=hfp3, in0=hb3, scalar=rowt[:, jt, 1:2], in1=hf3, op0=ALU.mult, op1=ALU.add), r=[HB["hwj"], b_const], w=[HB["hfp"]])
                    V(lambda e, jt=jt: e.tensor_scalar(out=hbp3, in0=hb3, scalar1=rowt[:, jt, 2:3], scalar2=None, op0=ALU.mult), r=[HB["hwj"], b_const], w=[HB["hbp"]])
                    V(lambda e, jt=jt: e.tensor_tensor(out=hp[:, jt, :], in0=hfp, in1=hbp, op=ALU.add), r=[HB["hfp"], HB["hbp"]], w=[HB["hp"]])
                    V(lambda e, jt=jt: e.tensor_tensor(out=hm[:, jt, :], in0=hfp, in1=hbp, op=ALU.subtract), r=[HB["hfp"], HB["hbp"]], w=[HB["hm"]])
                    A(lambda e: e.activation(out=hfp, in_=hfp, func=AF.Abs), r=[HB["hfp"]], w=[HB["hfp"]])
                    A(lambda e: e.activation(out=hbp, in_=hbp, func=AF.Abs), r=[HB["hbp"]], w=[HB["hbp"]])
                    V(lambda e, jt=jt: e.tensor_tensor(out=ab[:, jt, :], in0=hfp, in1=hbp, op=ALU.add), r=[HB["hfp"], HB["hbp"]], w=[HB["ab"]])
                    if jt in (0, 2, 4):
                        proj_typ(jt // 2)
                ps, bp = ps_next()
                for jt in range(8):
                    M(lambda e, ps=ps, jt=jt: e.matmul(ps[:, 0:256], lhsT=ones_b[:], rhs=ab[:, jt, :], start=(jt == 0), stop=(jt == 7)), r=[HB["ab"], b_const], w=[bp])
                V(lambda e, ps=ps: e.tensor_scalar(out=rinv, in0=ps[:, 0:256], scalar1=cfg[:, 2:3], scalar2=EPS, op0=ALU.mult, op1=ALU.add), r=[bp, b_const], w=[HB["rinv"]])
                V(lambda e: e.reciprocal(out=rinv, in_=rinv), r=[HB["rinv"]], w=[HB["rinv"]])
                FCv = FC_d.rearrange("(a p) f -> p a f", p=128); FSv = FS_d.rearrange("(a p) f -> p a f", p=128)
                for hh in range(2):
                    FCt, fcbuf = wload(FCv[:, :, hh * 512:(hh + 1) * 512], [128, 8, 512])
                    FSt, fsbuf = wload(FSv[:, :, hh * 512:(hh + 1) * 512], [128, 8, 512])
                    for ftl in range(4):
                        ft = hh * 4 + ftl
                        ps, bp = ps_next()
                        for jt in range(8):
                            M(lambda e, ps=ps, jt=jt, ftl=ftl, FCt=FCt: e.matmul(ps[:, 0:256], lhsT=FCt[:, jt, ftl * 128:(ftl + 1) * 128], rhs=hp[:, jt, :], start=(jt == 0), stop=(jt == 7)),
                              r=[fcbuf, HB["hp"]], w=[bp])
                        for jt in range(8):
                            M(lambda e, ps=ps, jt=jt, ftl=ftl, FSt=FSt: e.matmul(ps[:, 256:512], lhsT=FSt[:, jt, ftl * 128:(ftl + 1) * 128], rhs=hm[:, jt, :], start=False, stop=(jt == 7), skip_group_check=True),
                              r=[fsbuf, HB["hm"]], w=[bp])
                        for cs in range(2):
                            V(lambda e, ps=ps, ft=ft, cs=cs: e.scalar_tensor_tensor(out=Kp[:, ft, cs, :], in0=ps[:, cs * 256:(cs + 1) * 256], scalar=rowt[:, ft, 3:4], in1=rinv,
                                                                                   op0=ALU.mult, op1=ALU.mult), r=[bp, b_const, HB["rinv"]], w=[HB["Kp"]])
                for o in range(2):
                    zin, bzin = (vT, HB["vT"]) if o == 0 else (x1T, HB["x1T"])
                    gate, bgate = (x1T, HB["x1T"]) if o == 0 else (x2T, HB["x2T"])
                    A(lambda e, zin=zin: e.activation(out=zbf, in_=zin, func=AF.Identity), r=[bzin], w=[HB["zbf"]])
                    ps, bp = ps_next()
                    pb = psbf(ps)
                    for tt in range(8):
                        M(lambda e, pb=pb, tt=tt: e.transpose(out=pb[:, tt * 128:(tt + 1) * 128], in_=zbf[:, tt * 128:(tt + 1) * 128], identity=ident_b[:]), r=[HB["zbf"], b_const], w=[bp])
                    A(lambda e, pb=pb: e.activation(out=ztok.rearrange("p a b -> p (a b)"), in_=pb, func=AF.Identity), r=[bp], w=[HB["ztok"]])
                    psc_, bpc_ = ps_next(); pss_, bps_ = ps_next()
                    for (pz, bz, Mx) in ((psc_, bpc_, MC), (pss_, bps_, MS)):
                        for ft in range(8):
                            for tt in range(8):
                                M(lambda e, pz=pz, ft=ft, tt=tt, Mx=Mx: e.matmul(pz[:, ft * 128:(ft + 1) * 128], lhsT=Mx[:, tt, ft * 128:(ft + 1) * 128], rhs=ztok[:, tt, :], start=(tt == 0), stop=(tt == 7)),
                                  r=[b_MC, HB["ztok"]], w=[bz])
                    for hh in range(2):
                        Kc = Kp[:, hh * 4:hh * 4 + 4, 0, o * 128:(o + 1) * 128]; Ks = Kp[:, hh * 4:hh * 4 + 4, 1, o * 128:(o + 1) * 128]
                        Zc = psc_[:, hh * 512:(hh + 1) * 512].rearrange("p (a b) -> p a b", b=128); Zs = pss_[:, hh * 512:(hh + 1) * 512].rearrange("p (a b) -> p a b", b=128)
                        ta = t12[:, 0:512].rearrange("p (a b) -> p a b", b=128); tb = t12[:, 512:1024].rearrange("p (a b) -> p a b", b=128)
                        tc = t34[:, 0:512].rearrange("p (a b) -> p a b", b=128); td = t34[:, 512:1024].rearrange("p (a b) -> p a b", b=128)
                        V(lambda e, Zc=Zc, Kc=Kc, ta=ta: e.tensor_tensor(out=ta, in0=Zc, in1=Kc, op=ALU.mult), r=[bpc_, HB["Kp"]], w=[HB["t12"]])
                        V(lambda e, Zs=Zs, Ks=Ks, tb=tb: e.tensor_tensor(out=tb, in0=Zs, in1=Ks, op=ALU.mult), r=[bps_, HB["Kp"]], w=[HB["t12"]])
                        V(lambda e, ta=ta, tb=tb, hh=hh: e.tensor_tensor(out=Pq[:, hh * 4:hh * 4 + 4, :], in0=ta, in1=tb, op=ALU.subtract), r=[HB["t12"]], w=[HB["P"]])
                        V(lambda e, Zc=Zc, Ks=Ks, tc=tc: e.tensor_tensor(out=tc, in0=Zc, in1=Ks, op=ALU.mult), r=[bpc_, HB["Kp"]], w=[HB["t34"]])
                        V(lambda e, Zs=Zs, Kc=Kc, td=td: e.tensor_tensor(out=td, in0=Zs, in1=Kc, op=ALU.mult), r=[bps_, HB["Kp"]], w=[HB["t34"]])
                        V(lambda e, tc=tc, td=td, hh=hh: e.tensor_tensor(out=Qq[:, hh * 4:hh * 4 + 4, :], in0=tc, in1=td, op=ALU.add), r=[HB["t34"]], w=[HB["Q"]])
                    psy, bpy = ps_next()
                    for h in range(2):
                        for ft in range(8):
                            M(lambda e, psy=psy, h=h, ft=ft: e.matmul(psy[:, h * 512:(h + 1) * 512], lhsT=Pq[:, ft, :], rhs=MC[:, ft, h * 512:(h + 1) * 512], start=(ft == 0), stop=False),
                              r=[HB["P"], b_MC], w=[bpy])
                        for ft in range(8):
                            M(lambda e, psy=psy, h=h, ft=ft: e.matmul(psy[:, h * 512:(h + 1) * 512], lhsT=Qq[:, ft, :], rhs=MS[:, ft, h * 512:(h + 1) * 512], start=False, stop=(ft == 7)),
                              r=[HB["Q"], b_MC], w=[bpy])
                    V(lambda e, psy=psy, zin=zin, o=o, cc=cc: e.scalar_tensor_tensor(out=t12, in0=zin, scalar=hbias[:, o, cc:cc + 1], in1=psy[:], op0=ALU.mult, op1=ALU.add),
                      r=[bzin, b_lw, bpy], w=[HB["t12"]])
                    if o == 0:
                        V(lambda e: e.tensor_tensor(out=x1T, in0=t12, in1=x1T, op=ALU.mult), r=[HB["t12"], HB["x1T"]], w=[HB["x1T"]])
                    else:
                        V(lambda e, cc=cc: e.tensor_tensor(out=obT[:, cc, :], in0=t12, in1=x2T, op=ALU.mult), r=[HB["t12"], HB["x2T"]], w=[b_ob[cc]])
            dump("obT%d" % l, obT[:, 0, :], [b_ob[0]])
            dump("obT7_%d" % l, obT[:, 7, :], [b_ob[7]])

            GB = {k: Buf("hg_" + k) for k in ("qT", "sgT", "sig", "logf", "kk", "tmpE", "kblT", "Sall", "Sst", "attT", "t1")}
            GP = [{k: Buf("hg%d_%s" % (p_, k)) for k in ("qb", "qb32", "kb", "kbl", "kbl3", "edec", "vtok")} for p_ in range(2)]
            b_oa = [Buf("oa%d" % i) for i in range(8)]
            allg = list(GB.values()) + [b for d_ in GP for b in d_.values()] + b_oa
            alias(allg, list(HB.values()) + [b_MC, b_xt])
            qT = cf32(0, 1024); sgT = cf32(1024, 1024); sig = cf32(2048, 1024); logf = cf32(3072, 1024); kk = cf32(4096, 1024); tmpE = cf32(5120, 1024)
            kblT = cbf(7680, 1024)
            Sall = cbf(8704, 4096).rearrange("p (a b) -> p a b", b=128)
            Sst = cf32(10752, 128); attT = cbf(10944, 128); t1 = cf32(11008, 1024)
            oaT = cbf(12288, 8192).rearrange("p (a b) -> p a b", b=T)
            xtb2 = xt_t[:].bitcast(BF16)
            QB = [cbf(6656, 1024), ar2[:, 0:1024]]
            KB = [cbf(7168, 1024), ar2[:, 1024:2048]]
            KBL = [cbf(8192, 1024).rearrange("p (a b) -> p a b", b=128), ar2[:, 2048:3072].rearrange("p (a b) -> p a b", b=128)]
            KBL3 = [xtb2[:, 0:1024].rearrange("p (a b) -> p a b", b=128), xtb2[:, 1024:2048].rearrange("p (a b) -> p a b", b=128)]
            EDEC = [cf32(10880, 32), ar2[:, 3072:3136].bitcast(F32)]
            QB32 = [cf32(8704, 1024), cf32(9728, 1024)]
            RS = 16
            Sring = ar2[:, 5120:5120 + RS * 256].bitcast(F32).rearrange("p (a b) -> p a b", b=128)
            b_sr = [Buf("sr%d" % i) for i in range(RS)]
            Sbf = ar2[:, 9216:9216 + RS * 128].rearrange("p (a b) -> p a b", b=128)
            b_sb = [Buf("sb%d" % i) for i in range(RS)]
            alias(b_sr + b_sb, list(HB.values()) + [b_MC])
            VTOK = [cbf(6144, 1024).rearrange("p (a b) -> p a b", b=128), ar2[:, 4096:5120].rearrange("p (a b) -> p a b", b=128)]
            VBLK = [cbf(8704, 4096).rearrange("p (a j b) -> p a j b", j=4, b=128), ar2[:, 11264:15360].rearrange("p (a j b) -> p a j b", j=4, b=128)]
            b_vb = [Buf("vblk0"), Buf("vblk1")]
            alias(b_vb, list(HB.values()) + [b_MC])
            psn[0] = 1
            psO, bpO = pst[3], b_ps[3]
            b_psd = [Buf("psd0"), Buf("psd1")]
            alias(b_psd, [b_ps[2]])
            dsc = [0]
            iters = [(h_, d_) for h_ in range(8) for d_ in range(2)]

            def prep_thunks(it):
                h, dr = iters[it]
                p_ = it % 2
                hp_ = h % 2
                gp = GP[p_]
                qb, kb, kbl, kbl3, edec = QB[p_], KB[p_], KBL[p_], KBL3[p_], EDEC[p_]
                qb32 = QB32[p_]
                vtok = VTOK[hp_]; bvt = GP[hp_]["vtok"]
                idx = dr * 8 + h
                th = []
                if dr == 0:
                    def t_q():
                        wt, wb = wchunk(w_in, h * 128)
                        ps, bp = ps_next(); proj_fm(wt, 0, 16, hT, b_hT, ps, bp, wb)
                        A(lambda e: e.activation(out=qT, in_=ps[:], func=AF.Silu), r=[bp], w=[GB["qT"]])
                    th.append(t_q)

                    def t_v():
                        wt, wb = wchunk(w_in, (24 + h) * 128)
                        ps, bp = ps_next()
                        for tt in range(8):
                            for c in range(16):
                                M(lambda e, tt=tt, c=c: e.matmul(ps[:, tt * 128:(tt + 1) * 128], lhsT=hT[:, c, tt * 128:(tt + 1) * 128], rhs=wt[:, c, :], start=(c == 0), stop=(c == 15)),
                                  r=[wb, b_hT[c]], w=[bp])
                        A(lambda e: e.activation(out=vtok.rearrange("p a b -> p (a b)"), in_=ps[:], func=AF.Identity), r=[bp], w=[bvt])
                        vb = VBLK[hp_]
                        for j in range(4):
                            G(lambda e, j=j: e.tensor_scalar(out=vb[:, :, j, :], in0=vtok, scalar1=rmask[:, j:j + 1], scalar2=1.0, op0=ALU.mult, op1=ALU.mult), r=[bvt, b_const], w=[b_vb[hp_]])
                    th.append(t_v)

                def t_f():
                    wt, wb = wchunk(w_in, (8 + 8 * dr + h) * 128)
                    ps, bp = ps_next(); proj_fm(wt, 0, 16, hT, b_hT, ps, bp, wb)
                    A(lambda e: e.activation(out=sig, in_=ps[:], func=AF.Sigmoid), r=[bp], w=[GB["sig"]])
                th.append(t_f)

                def t_ln():
                    A(lambda e: e.activation(out=logf, in_=sig, func=AF.Ln, scale=lbt[:, 1, idx:idx + 1], bias=lbt[:, 0, idx:idx + 1]), r=[GB["sig"], b_lbt], w=[GB["logf"]])
                    V(lambda e: e.tensor_scalar(out=kk, in0=sig, scalar1=lbt[:, 2, idx:idx + 1], scalar2=lbt[:, 1, idx:idx + 1], op0=ALU.mult, op1=ALU.add),
                      r=[GB["sig"], b_lbt], w=[GB["kk"]])
                th.append(t_ln)
                if dr == 0:
                    bb, bbb, anc = sig, GB["sig"], 31
                else:
                    bb, bbb, anc = logf, GB["logf"], 0
                bb3 = bb.rearrange("p (n i) -> p n i", i=32)

                def t_scan():
                    V(lambda e: e.tensor_tensor_scan(out=sig, data0=tmask[:, 0, :], data1=logf, initial=0.0, op0=ALU.mult, op1=ALU.add),
                      r=[GB["logf"], b_const], w=[GB["sig"]])
                    if dr == 1:
                        V(lambda e: e.scalar_tensor_tensor(out=tmpE, in0=sig, scalar=-1.0, in1=logf, op0=ALU.mult, op1=ALU.add), r=[GB["sig"], GB["logf"]], w=[GB["tmpE"]])
                        s3 = sig.rearrange("p (n i) -> p n i", i=32)
                        V(lambda e: e.tensor_tensor(out=logf.rearrange("p (n i) -> p n i", i=32), in0=tmpE.rearrange("p (n i) -> p n i", i=32),
                                                    in1=s3[:, :, 31:32].to_broadcast([128, 32, 32]), op=ALU.add), r=[GB["tmpE"], GB["sig"]], w=[GB["logf"]])
                th.append(t_scan)

                def t_q2():
                    A(lambda e: e.activation(out=edec.rearrange("p (n i) -> p n i", i=1), in_=bb3[:, :, anc:anc + 1], func=AF.Exp), r=[bbb], w=[gp["edec"]])
                    A(lambda e: e.activation(out=tmpE, in_=bb, func=AF.Exp), r=[bbb], w=[GB["tmpE"]])
                    V(lambda e: e.tensor_tensor(out=qb, in0=qT, in1=tmpE, op=ALU.mult), r=[GB["qT"], GB["tmpE"]], w=[gp["qb"]])
                th.append(t_q2)

                def t_k2():
                    A(lambda e: e.activation(out=tmpE, in_=bb, func=AF.Exp, scale=-1.0), r=[bbb], w=[GB["tmpE"]])
                    V(lambda e: e.tensor_tensor(out=kb, in0=kk, in1=tmpE, op=ALU.mult), r=[GB["kk"], GB["tmpE"]], w=[gp["kb"]])
                th.append(t_k2)

                def t_kl():
                    V(lambda e: e.tensor_tensor(out=tmpE.rearrange("p (n i) -> p n i", i=32), in0=bb3[:, :, anc:anc + 1].to_broadcast([128, 32, 32]), in1=bb3, op=ALU.subtract),
                      r=[bbb], w=[GB["tmpE"]])
                    A(lambda e: e.activation(out=tmpE, in_=tmpE, func=AF.Exp), r=[GB["tmpE"]], w=[GB["tmpE"]])
                    V(lambda e: e.tensor_tensor(out=kblT, in0=kk, in1=tmpE, op=ALU.mult), r=[GB["kk"], GB["tmpE"]], w=[GB["kblT"]])
                th.append(t_kl)

                def t_tr():
                    ps, bp = ps_next(); pb = psbf(ps)
                    for tt in range(8):
                        M(lambda e, tt=tt: e.transpose(out=pb[:, tt * 128:(tt + 1) * 128], in_=kblT[:, tt * 128:(tt + 1) * 128], identity=ident_b[:]), r=[GB["kblT"], b_const], w=[bp])
                    A(lambda e: e.activation(out=kbl.rearrange("p a b -> p (a b)"), in_=pb, func=AF.Identity), r=[bp], w=[gp["kbl"]])
                th.append(t_tr)
                return th

            psd_of = {}

            def dS_emit(it, tt):
                h, dr = iters[it]
                p_ = it % 2
                gp = GP[p_]
                qb, kb, kbl, kbl3, edec = QB[p_], KB[p_], KBL[p_], KBL3[p_], EDEC[p_]
                qb32 = QB32[p_]
                vtok = VTOK[h % 2]; bvt = GP[h % 2]["vtok"]
                di = dsc[0] % 2
                dsc[0] += 1
                psd, bpd = pst[2][:, di * 512:(di + 1) * 512], b_psd[di]
                psd_of[(it, tt)] = (psd, bpd)
                vb = VBLK[h % 2]
                M(lambda e: e.matmul(psd[:, 0:512], lhsT=kbl[:, tt, :], rhs=vb[:, tt, :, :].rearrange("p j b -> p (j b)"), start=True, stop=True),
                  r=[gp["kbl"], b_vb[h % 2]], w=[bpd], acc=False)

            def tile_rest(it, tt):
                h, dr = iters[it]
                p_ = it % 2
                gp = GP[p_]
                qb, kb, kbl, kbl3, edec = QB[p_], KB[p_], KBL[p_], KBL3[p_], EDEC[p_]
                vtok = VTOK[h % 2]; bvt = GP[h % 2]["vtok"]
                tri = trif if dr == 0 else trib
                psd, bpd = psd_of.pop((it, tt))
                chunks = range(4) if dr == 0 else range(3, -1, -1)
                for j in chunks:
                    n = tt * 4 + j
                    step = n if dr == 0 else 31 - n
                    ec, en = step % RS, (step + 1) % RS
                    A(lambda e, ec=ec: e.activation(out=Sbf[:, ec, :], in_=Sring[:, ec, :], func=AF.Identity), r=[b_sr[ec]], w=[b_sb[ec]])
                    V(lambda e, n=n, j=j, ec=ec, en=en: e.scalar_tensor_tensor(out=Sring[:, en, :], in0=Sring[:, ec, :], scalar=edec[:, n:n + 1], in1=psd[:, j * 128:(j + 1) * 128],
                                                                              op0=ALU.mult, op1=ALU.add), r=[b_sr[ec], gp["edec"], bpd], w=[b_sr[en]])
                    segend = (n % 8 == 7) if dr == 0 else (n % 8 == 0)
                    if segend:
                        ld(st_d[l, n // 8, dr, h], Sring[:, en, :], [], r=[b_sr[en]])
                        last = (n == 31) if dr == 0 else (n == 0)
                        if not last:
                            V(lambda e, en=en: e.tensor_scalar(out=Sring[:, en, :], in0=Sring[:, en, :], scalar1=cfg[:, 0:1], scalar2=None, op0=ALU.mult), r=[b_sr[en], b_const], w=[b_sr[en]])
                psa, bpa = pst[1], b_ps[1]
                M(lambda e: e.matmul(psa[:, 0:128], lhsT=kb[:, tt * 128:(tt + 1) * 128], rhs=qb[:, tt * 128:(tt + 1) * 128], start=True, stop=True),
                  r=[gp["kb"], gp["qb"]], w=[bpa], acc=False)
                V(lambda e: e.tensor_tensor(out=attT, in0=psa[:, 0:128], in1=tri[:], op=ALU.mult), r=[bpa, b_const], w=[GB["attT"]])
                first = (dr == 0 and tt in (0, 4))
                M(lambda e: e.matmul(psO[:, tt * 128:(tt + 1) * 128], lhsT=vtok[:, tt, :], rhs=attT, start=first, stop=False, skip_group_check=True),
                  r=[bvt, GB["attT"]], w=[bpO])
                for j in range(4):
                    n = tt * 4 + j
                    step = n if dr == 0 else 31 - n
                    ec = step % RS
                    M(lambda e, n=n, j=j, ec=ec: e.matmul(psO[:, n * 32:(n + 1) * 32], lhsT=Sbf[:, ec, :], rhs=qb[:, n * 32:(n + 1) * 32], start=False, stop=(dr == 1 and tt == 0 and j == 3), skip_group_check=True),
                      r=[b_sb[ec], gp["qb"]], w=[bpO])

            def head_finish(h):
                wt, wb = wchunk(w_in, (32 + h) * 128)
                ps, bp = ps_next(); proj_fm(wt, 0, 16, hT, b_hT, ps, bp, wb)
                A(lambda e: e.activation(out=sgT, in_=ps[:], func=AF.Silu), r=[bp], w=[GB["sgT"]])
                i = sqc[0] % 2; sqc[0] += 1
                A(lambda e: e.activation(out=sq[i][:], in_=psO[:], func=AF.Square), r=[bpO], w=[b_sq[i]])
                pss, bpss = ps_next()
                for hf_ in range(2):
                    M(lambda e, hf_=hf_: e.matmul(pss[:, hf_ * 512:(hf_ + 1) * 512], lhsT=ones_b[:], rhs=sq[i][:, hf_ * 512:(hf_ + 1) * 512], start=True, stop=True),
                      r=[b_sq[i], b_const], w=[bpss], acc=False)
                A(lambda e: e.activation(out=rstd[:], in_=pss[:], func=AF.Sqrt, scale=1.0 / 128, bias=epsb[:]), r=[bpss, b_const], w=[b_rstd])
                V(lambda e: e.reciprocal(out=rstd[:], in_=rstd[:]), r=[b_rstd], w=[b_rstd])
                V(lambda e: e.scalar_tensor_tensor(out=t1, in0=psO[:], scalar=hnorm[:, 0:1], in1=rstd[:], op0=ALU.mult, op1=ALU.mult), r=[bpO, b_lw, b_rstd], w=[GB["t1"]])
                V(lambda e: e.tensor_tensor(out=oaT[:, h, :], in0=t1, in1=sgT, op=ALU.mult), r=[GB["t1"], GB["sgT"]], w=[b_oa[h]])

            for f_ in prep_thunks(0):
                f_()
            for it in range(16):
                h, dr = iters[it]
                nxt = prep_thunks(it + 1) if it + 1 < 16 else []
                ld(Sring[:, 0, :], s0_d[l, dr, h], [b_sr[0]])
                tiles = list(range(8)) if dr == 0 else list(range(7, -1, -1))
                per = (len(nxt) + 7) // 8
                dS_emit(it, tiles[0])
                for k_, tt in enumerate(tiles):
                    if k_ + 1 < 8:
                        dS_emit(it, tiles[k_ + 1])
                    tile_rest(it, tt)
                    for f_ in nxt[k_ * per:(k_ + 1) * per]:
                        f_()
                for f_ in nxt[8 * per:]:
                    f_()
                if dr == 1:
                    head_finish(h)
            psn[0] = 4
            alias([b_ps[2]], b_psd)
            dump("oaT%d" % l, oaT[:, 0, :], [b_oa[0]])
            dump("oaT7_%d" % l, oaT[:, 7, :], [b_oa[7]])

            alias(b_mg, [b_MC] + allg + b_sr + b_sb + b_vb)
            alias([b_arena, b_xt], allg)
            wba = wba_d[l]; wbb = wbb_d[l]
            for j in range(16):
                for br, (wbr2d, oT, boT, gbase) in enumerate(((wba, oaT, b_oa, 8192), (wbb, obT, b_ob, 10240))):
                    wg, wgb = wchunk(w_in, gbase + j * 128)
                    psg, bpg = ps_next(); proj_fm(wg, 0, 16, hT, b_hT, psg, bpg, wgb)
                    A(lambda e, psg=psg: e.activation(out=tmpn, in_=psg[:], func=AF.Sigmoid), r=[bpg], w=[b_arena])
                    wr, wrb = wchunk(wbr2d, j * 128, KC=8)
                    psb_, bpb_ = ps_next(); proj_fm(wr, 0, 8, oT, boT, psb_, bpb_, wrb)
                    if br == 0:
                        V(lambda e, psb_=psb_: e.tensor_tensor(out=xt_t[:], in0=psb_[:], in1=tmpn, op=ALU.mult), r=[bpb_, b_arena], w=[b_xt])
                    else:
                        V(lambda e, psb_=psb_: e.tensor_tensor(out=tmpn, in0=psb_[:], in1=tmpn, op=ALU.mult), r=[bpb_, b_arena], w=[b_arena])
                        V(lambda e, j=j: e.tensor_tensor(out=merged[:, j, :], in0=tmpn, in1=xt_t[:], op=ALU.add), r=[b_arena, b_xt], w=[b_mg[j]])
            dump("mg%d" % l, merged[:, 0, :], [b_mg[0]])
            alias(b_big, allg + b_vb)
            wout = wout_d[l]
            for jp in range(8):
                wt, wb = wload(kview(wout, jp * 256, jp * 256 + 256, 16), [128, 16, 256])
                for fi in range(2):
                    j = jp * 2 + fi
                    psw, bpw = ps_next(); proj_fm(wt, fi * 128, 16, merged, b_mg, psw, bpw, wb)
                    A(lambda e, psw=psw, j=j: e.activation(out=big[:, j, :], in_=psw[:], func=AF.Identity), r=[bpw], w=[b_big[j]])
            S = sumsq_begin()
            for c in range(16):
                sumsq_add(S, big[:, c, :], [b_big[c]], 16)
            sumsq_finish(S, D)
            residual_apply(2)
            dump("x1_%d" % l, big[:, 0, :], [b_big[0]])

            S = sumsq_begin()
            for c in range(16):
                sumsq_add(S, big[:, c, :], [b_big[c]], 16)
            sumsq_finish(S, D)
            norm_apply(3, 4)
            for c in range(16):
                ld(xs_d[:, c, :], big[:, c, :], [], r=[b_big[c]])
            alias([b_U, b_dg[0], b_dg[1], b_sa, b_gT[0], b_gT[1]], b_mg)
            G(lambda e: e.memset(Uvar, 0.0), w=[b_U])
            wup = wup_d[l]; wdn = wdn_d[l]
            b_fs = [Buf("fs0"), Buf("fs1")]
            alias(b_fs, b_ob)
            fslots = [(wring[i][:, :], b_wr[i]) for i in range(NW)] + [(ar2[:, 16384 + i * 4096:16384 + (i + 1) * 4096], b_fs[i]) for i in range(2)]
            fctr = [0]

            def fload(src_ap, shape):
                i = fctr[0] % len(fslots)
                fctr[0] += 1
                flat, buf = fslots[i]
                n = 1
                for s_ in shape[1:]:
                    n *= s_
                view = flat[:, 0:n]
                if len(shape) == 3:
                    view = view.rearrange("p (a b) -> p a b", b=shape[2])
                P.dma("pool", lambda e: e.dma_start(out=view, in_=src_ap), writes=[buf])
                return view, buf

            units = [(g_, fi_, ty_) for g_ in range(NFC // 2) for fi_ in range(2) for ty_ in range(2)]
            wt_of = {}
            psu_of = {}
            dgc = [0]

            def ensure_w(u):
                g_, fi_, ty_ = u
                if (g_, ty_) not in wt_of:
                    c0 = ty_ * DFF + g_ * 256
                    wt_of[(g_, ty_)] = fload(kview(wup, c0, c0 + 256, 16), [128, 16, 256])

            def S1(u):
                g_, fi_, ty_ = u
                wt, wb = wt_of[(g_, ty_)]
                psu, bpu = ps_next()
                proj_fm(wt, fi_ * 128, 16, hT, b_hT, psu, bpu, wb)
                psu_of[u] = (psu, bpu)
                pinned.add(pst.index(psu))

            def S234(u):
                g_, fi_, ty_ = u
                j = g_ * 2 + fi_
                ci = ty_ * NFC + j
                gs = g_ % 2
                psu, bpu = psu_of.pop(u)
                A(lambda e: e.activation(out=Uvar[:, 0, PAD:PAD + T], in_=psu[:], func=AF.Identity), r=[bpu], w=[b_U])
                V(lambda e: e.tensor_tensor(out=Uvar[:, 1, PAD:PAD + T], in0=psu[:], in1=tmask[:, 2, :], op=ALU.mult), r=[bpu, b_const], w=[b_U])
                V(lambda e: e.tensor_tensor(out=Uvar[:, 2, PAD:PAD + T], in0=psu[:], in1=tmask[:, 3, :], op=ALU.mult), r=[bpu, b_const], w=[b_U])
                pinned.discard(pst.index(psu))
                dg = diag[dgc[0] % 2]; bdg = b_dg[dgc[0] % 2]; dgc[0] += 1
                for k in range(9):
                    V(lambda e, k=k: e.tensor_scalar(out=dg[:, k, :], in0=ident_b[:], scalar1=fcw[:, ci, k:k + 1], scalar2=None, op0=ALU.mult),
                      r=[b_const, b_lw], w=[bdg])
                psc, bpc = ps_next()
                for k in range(9):
                    ky, kx = k // 3, k % 3
                    dl = (ky - 1) * 64 + (kx - 1)
                    var = (1, 0, 2)[kx]
                    for h in range(2):
                        o0 = PAD + h * 512 + dl
                        M(lambda e, k=k, h=h, var=var, o0=o0: e.matmul(psc[:, h * 512:(h + 1) * 512], lhsT=dg[:, k, :], rhs=Uvar[:, var, o0:o0 + 512],
                                                                      start=(k == 0), stop=(k == 8)), r=[bdg, b_U], w=[bpc])
                if ty_ == 0:
                    A(lambda e: e.activation(out=sa_t, in_=psc[:], func=AF.Silu, bias=fcb[:, ci:ci + 1], scale=1.0), r=[bpc, b_lw], w=[b_sa])
                else:
                    V(lambda e: e.scalar_tensor_tensor(out=gT[gs][:, fi_, :], in0=psc[:], scalar=fcb[:, ci:ci + 1], in1=sa_t,
                                                       op0=ALU.add, op1=ALU.mult), r=[bpc, b_lw, b_sa], w=[b_gT[gs]])

            def downproj(g_):
                gs = g_ % 2
                wds = []
                for fi_ in range(2):
                    j = g_ * 2 + fi_
                    wds.append(fload(wdn[j * 128:(j + 1) * 128, :], [128, 2048]))
                for n in range(16):
                    psd, bpd = ps_next()
                    for fi_ in range(2):
                        for h in range(2):
                            M(lambda e, n=n, fi_=fi_, h=h, psd=psd, wd=wds[fi_][0]: e.matmul(psd[:, h * 512:(h + 1) * 512], lhsT=wd[:, n * 128:(n + 1) * 128],
                                                                                           rhs=gT[gs][:, fi_, h * 512:(h + 1) * 512], start=(fi_ == 0), stop=(fi_ == 1)),
                              r=[wds[fi_][1], b_gT[gs]], w=[bpd])
                    if g_ == 0:
                        V(lambda e, n=n, psd=psd: e.tensor_copy(out=big[:, n, :], in_=psd[:]), r=[bpd], w=[b_big[n]])
                    else:
                        V(lambda e, n=n, psd=psd: e.tensor_tensor(out=big[:, n, :], in0=psd[:], in1=big[:, n, :], op=ALU.add), r=[bpd, b_big[n]], w=[b_big[n]])

            b_fs_prev[:] = b_fs
            ensure_w(units[0]); S1(units[0])
            for ui, u in enumerate(units):
                if ui + 1 < len(units):
                    ensure_w(units[ui + 1]); S1(units[ui + 1])
                S234(u)
                if u[1] == 1 and u[2] == 1:
                    downproj(u[0])
            S = sumsq_begin()
            for c in range(16):
                sumsq_add(S, big[:, c, :], [b_big[c]], 16)
            sumsq_finish(S, D)
            residual_apply(5)
            dump("xo%d" % l, big[:, 0, :], [b_big[0]])

        yv = yT_d.rearrange("(c p) t -> p c t", p=128)
        for c in range(16):
            ld(yv[:, c, :], big[:, c, :], [], r=[b_big[c]])
        P.wait_all("sp", b_big + [b_const])
        P.emit()
    return nc


def _tables(is_sample):
    L = 1024 if is_sample else 256
    nseg = T // L
    t = np.arange(T)
    tl = t % L
    a = (tl[:, None] + 0.5) * (tl[None, :] + 0.5) * (math.pi / L)
    same = (t[:, None] // L) == (t[None, :] // L)
    MC = np.where(same, np.cos(a), 0.0).astype(np.float32)
    MS = np.where(same, np.sin(a), 0.0).astype(np.float32)
    b = tl[:, None] * (tl[None, :] + 0.5) * (math.pi / L)
    FC = np.where(same, np.cos(b), 0.0).astype(np.float32)
    FS = np.where(same, np.sin(b), 0.0).astype(np.float32)
    t01 = tl / max(L - 1, 1)
    m0 = (tl == 0).astype(np.float64)
    rowt = np.stack([t01, m0, 1.0 - m0, np.full(T, 1.0 / L)], axis=-1)
    rowt = rowt.reshape(8, 128, 4).transpose(1, 0, 2).astype(np.float32)
    cmask_f = (t % 32 != 0).astype(np.float32)
    if is_sample:
        maskL = (t % 64 != 63); maskR = (t % 64 != 0)
    else:
        maskL = (t % 256 != 255); maskR = (t % 256 != 0)
    tmask = np.stack([cmask_f, cmask_f, maskL.astype(np.float32), maskR.astype(np.float32)], 0).astype(np.float32)
    bands = np.linspace(1e-4, 16 - 1, 16).astype(np.float32)
    ang = (2.0 * math.pi / L) * tl[:, None].astype(np.float32) * bands[None, :]
    feats = np.concatenate([t01[:, None].astype(np.float32), np.cos(ang), -np.sin(ang)], axis=-1).astype(np.float32)
    cf = 1.0 if is_sample else 0.0
    cfg = np.zeros((128, 8), np.float32)
    cfg[:, 0] = cf; cfg[:, 1] = 1.0 - cf; cfg[:, 2] = 1.0 / nseg
    p = np.arange(128)
    same_c = (p[:, None] // 32) == (p[None, :] // 32)
    trif = (same_c & (p[:, None] <= p[None, :])).astype(np.float32)
    trib = (same_c & (p[:, None] >= p[None, :])).astype(np.float32)
    return dict(cfg=cfg, rowt=rowt, tmask=tmask, featsT=np.ascontiguousarray(feats.T), MC=MC, MS=MS, FC=FC, FS=FS,
                trif=trif, trib=trib)


def prep_inputs(x_prompt, x_sample, state_hgrn, c, c_ctx, w_mod, b_mod, g_pre_mix, g_post_mix,
                g_pre_ffn, g_post_ffn, w_in, hgrn_lower_bounds, hgrn_norm, hy_conv_w, hy_conv_b,
                hy_w1, hy_b1, hy_w2, hy_b2, hy_w3, hy_decay, hy_bias, w_branch_a, w_branch_b,
                w_out, ffn_w_up, ffn_conv_w, ffn_conv_b, ffn_w_down):
    f = lambda a: np.ascontiguousarray(np.asarray(a, dtype=np.float32))

    def pc(v, n):
        v = f(v)
        return np.ascontiguousarray(np.swapaxes(v.reshape(v.shape[:-1] + (n, 128)), -1, -2))

    shared = dict(
        w_mod=f(w_mod), b_modT=pc(b_mod, 96),
        gvec=np.ascontiguousarray(np.stack([pc(g_pre_mix, 16), pc(g_post_mix, 16), pc(g_pre_ffn, 16), pc(g_post_ffn, 16)], axis=2)),
        w_in=f(w_in),
        hlb=np.ascontiguousarray(pc(f(hgrn_lower_bounds).reshape(DEPTH, 2048), 16)),
        hnorm=f(hgrn_norm).reshape(DEPTH, 128, 1),
        hcw=np.ascontiguousarray(pc(f(hy_conv_w), 24).transpose(0, 2, 3, 1)),
        hcb=pc(hy_conv_b, 24),
        hw1=f(hy_w1), hb1=f(hy_b1).reshape(DEPTH, 64, 1), hw2=f(hy_w2), hb2=f(hy_b2).reshape(DEPTH, 64, 1), hw3=f(hy_w3),
        hdec=f(hy_decay), hbias=np.ascontiguousarray(pc(f(hy_bias), 8).transpose(0, 2, 1, 3)),
        wba=f(w_branch_a), wbb=f(w_branch_b), wout=f(w_out), wup=f(ffn_w_up),
        fcw=np.ascontiguousarray(pc(f(ffn_conv_w).reshape(DEPTH, 9, 2 * DFF), 88).transpose(0, 2, 3, 1)),
        fcb=pc(ffn_conv_b, 88), wdn=f(ffn_w_down),
    )
    tabs = {True: _tables(True), False: _tables(False)}
    xs = f(x_sample); xp = f(x_prompt); sh = f(state_hgrn)
    maps = []
    for core in range(8):
        is_s = core < 4
        m = dict(shared)
        m.update(tabs[is_s])
        if is_s:
            m["xT"] = np.ascontiguousarray(xs[core].T)
            m["s0"] = np.ascontiguousarray(sh[core])
            m["cond"] = pc(f(c)[core], 16)
        else:
            j = core - 4
            m["xT"] = np.ascontiguousarray(xp[4 * j:4 * j + 4].reshape(T, D).T)
            m["s0"] = np.zeros((DEPTH, 2, 8, 128, 128), np.float32)
            m["cond"] = pc(f(c_ctx), 16)
        maps.append(m)
    return maps


_NC = [None]


def kernel(**inputs):
    maps = prep_inputs(**inputs)
    if _NC[0] is None:
        _NC[0] = build()
    res = run_bass_kernel_spmd(_NC[0], maps, core_ids=list(range(8)))
    R = res.results
    y_sample = np.stack([np.ascontiguousarray(R[i]["yT"].T) for i in range(4)], 0).astype(np.float32)
    y_prompt = np.concatenate([np.ascontiguousarray(R[4 + j]["yT"].T).reshape(4, 256, D) for j in range(4)], 0).astype(np.float32)
    ns = np.concatenate([np.ascontiguousarray(R[4 + j]["st"].transpose(1, 0, 2, 3, 4, 5)) for j in range(4)], 0).astype(np.float32)
    return (y_prompt, y_sample, ns)
```
